# Optimizing a Trainium2 kernel written in Bass

```python
import jax
import jax.numpy as jnp
from jax import lax
import numpy as np

D_MODEL = 4096
BATCH = 32
SEQ = 256
DEPTH = 2
DEC_BATCH = 2
DEC_SEQ = 1024
PAST_LEN = 256

GRID_W = 64
N_MOD = 6
NORM_EPS = 1e-6
GN_EPS = 64e-5
HEAD_DIM = 128
ATT_HEADS = 16
ATT_KV_HEADS = 4
ATT_GROUP = ATT_HEADS // ATT_KV_HEADS
Q_BLOCK = 128
ROPE_THETA = 10000.0
M_HEADS = 4
M_DQK = 128
M_DV = 256
M_CHUNK = 64
R_HEADS = 16
R_HD = 64
DECAY_LORA = 64
AAA_LORA = 64
GATE_LORA = 128
ATT_W = ATT_HEADS * HEAD_DIM
KV_W = ATT_KV_HEADS * HEAD_DIM
M_QK_W = M_HEADS * M_DQK
M_V_W = M_HEADS * M_DV
M_GATE_W = 4 * M_HEADS
R_W = R_HEADS * R_HD
R_LORA_W = 2 * DECAY_LORA + 2 * AAA_LORA + GATE_LORA
R_COLS = 3 * R_W + R_LORA_W
MIX_W = ATT_W + M_V_W + R_W
IN_SIZES = (ATT_W, KV_W, KV_W, M_QK_W, M_QK_W, M_V_W, M_V_W, M_GATE_W, R_COLS)
IN_COLS = ATT_W + 2 * KV_W + 2 * M_QK_W + 2 * M_V_W + M_GATE_W + R_COLS
R_SIZES = (R_W, R_W, R_W, DECAY_LORA, DECAY_LORA, AAA_LORA, AAA_LORA, GATE_LORA)
D_FF = ((8 * D_MODEL + 3 * 256 - 1) // (3 * 256)) * 256
F32 = jnp.float32

kernel_name = 'hybrid_diffusion_attn_mlstm_rwkv7_step'


def _split(x, sizes):
    idx = [int(i) for i in np.cumsum(sizes)[:-1]]
    return jnp.split(x, idx, axis=-1)


def rmsnorm(x, g, eps=NORM_EPS):
    xf = x.astype(F32)
    y = xf * lax.rsqrt(jnp.mean(xf * xf, axis=-1, keepdims=True) + eps)
    return (y * g.astype(F32)).astype(x.dtype)


def axial_rope(rows):
    row = jnp.repeat(jnp.arange(rows), GRID_W).astype(F32)
    col = jnp.tile(jnp.arange(GRID_W), rows).astype(F32)
    axis_dim = HEAD_DIM // 2
    inv = ROPE_THETA ** (-jnp.arange(0, axis_dim, 2, dtype=F32) / axis_dim)
    ang = jnp.concatenate([row[:, None] * inv, col[:, None] * inv], axis=-1)
    return jnp.cos(ang), jnp.sin(ang)


def apply_rope(x, cos, sin):
    xf = x.astype(F32)
    x1, x2 = jnp.split(xf, 2, axis=-1)
    c = cos[None, :, None, :]
    s = sin[None, :, None, :]
    return jnp.concatenate([x1 * c - x2 * s, x2 * c + x1 * s], axis=-1).astype(x.dtype)


def block_attention(q, k, v):
    b, t = q.shape[:2]
    nb = t // Q_BLOCK
    qb = q.reshape(b, nb, Q_BLOCK, ATT_KV_HEADS, ATT_GROUP, HEAD_DIM).swapaxes(0, 1)
    kf = k.astype(F32)
    vf = v.astype(F32)
    scale = HEAD_DIM ** -0.5

    def one_block(qblk):
        s = jnp.einsum('bqkgd,bskd->bkgqs', qblk.astype(F32), kf) * scale
        p = jax.nn.softmax(s, axis=-1)
        return jnp.einsum('bkgqs,bskd->bqkgd', p, vf)

    o = lax.map(one_block, qb)
    return o.swapaxes(0, 1).reshape(b, t, ATT_W).astype(q.dtype)


def mlstm_scan(q, k, v, li, lf, c0, n0, m0):
    b, t = q.shape[:2]
    nc = t // M_CHUNK

    def chunks(a):
        return a.reshape((b, nc, M_CHUNK) + a.shape[2:]).swapaxes(0, 1)

    tri = jnp.tril(jnp.ones((M_CHUNK, M_CHUNK), dtype=bool))[None, :, :, None]

    def body(carry, xs):
        cm, nm, mm = carry
        qc, kc, vc, ic, fc = xs
        bcum = jnp.cumsum(fc, axis=1)
        log_d = jnp.where(tri, bcum[:, :, None, :] - bcum[:, None, :, :] + ic[:, None, :, :], -jnp.inf)
        m_inter = bcum + mm[:, None, :]
        m_j = jnp.maximum(jnp.max(log_d, axis=2), m_inter)
        dw = jnp.exp(log_d - m_j[:, :, None, :])
        w = jnp.einsum('bjhd,bshd->bjsh', qc, kc) * dw
        g_inter = jnp.exp(m_inter - m_j)
        num = jnp.einsum('bjsh,bshv->bjhv', w, vc) + g_inter[..., None] * jnp.einsum('bjhd,bhdv->bjhv', qc, cm)
        den = jnp.sum(w, axis=2) + g_inter * jnp.einsum('bjhd,bhd->bjh', qc, nm)
        h = num / jnp.maximum(jnp.abs(den), jnp.exp(-m_j))[..., None]
        b_last = bcum[:, -1, :]
        log_u = b_last[:, None, :] - bcum + ic
        m_new = jnp.maximum(b_last + mm, jnp.max(log_u, axis=1))
        u = jnp.exp(log_u - m_new[:, None, :])
        carry_decay = jnp.exp(b_last + mm - m_new)
        c_new = carry_decay[..., None, None] * cm + jnp.einsum('bsh,bshd,bshv->bhdv', u, kc, vc)
        n_new = carry_decay[..., None] * nm + jnp.einsum('bsh,bshd->bhd', u, kc)
        return (c_new, n_new, m_new), h

    (cf, nf, mf), hs = lax.scan(body, (c0, n0, m0), tuple(chunks(a) for a in (q, k, v, li, lf)))
    return hs.swapaxes(0, 1).reshape((b, t) + hs.shape[3:]), cf, nf, mf


def wkv7_scan(r, w, k, v, a, bb, s0):
    seq = tuple(jnp.swapaxes(z, 0, 1) for z in (r, w, k, v, a, bb))

    def step(s, inp):
        r_t, w_t, k_t, v_t, a_t, b_t = inp
        sa = jnp.einsum('bhvk,bhk->bhv', s, a_t)
        s = s * w_t[:, :, None, :] + sa[..., None] * b_t[:, :, None, :] + v_t[..., None] * k_t[:, :, None, :]
        return s, jnp.einsum('bhvk,bhk->bhv', s, r_t)

    s_fin, ys = lax.scan(step, s0, seq)
    return jnp.swapaxes(ys, 0, 1), s_fin


def centred_shift_mix(x, mu):
    xp = jnp.pad(x, ((0, 0), (1, 1), (0, 0)))
    return x + (0.5 * (xp[:, :-2] + xp[:, 2:]) - x) * mu


def mixers(h, lp, rope, ctx_kv, init):
    b, t, _ = h.shape
    aq, ak, av, mq, mk, mv, mo, mg, rp = _split(h @ lp['w_in'], IN_SIZES)

    q = rmsnorm(aq.reshape(b, t, ATT_HEADS, HEAD_DIM), lp['g_qk'][0])
    k = rmsnorm(ak.reshape(b, t, ATT_KV_HEADS, HEAD_DIM), lp['g_qk'][1])
    v = av.reshape(b, t, ATT_KV_HEADS, HEAD_DIM)
    if ctx_kv is None:
        att = block_attention(q, k, v)
    else:
        cos, sin = rope
        keys = jnp.concatenate([ctx_kv[0].astype(k.dtype), apply_rope(k, cos, sin)], axis=1)
        vals = jnp.concatenate([ctx_kv[1].astype(v.dtype), v], axis=1)
        att = block_attention(apply_rope(q, cos, sin), keys, vals)

    qm = mq.reshape(b, t, M_HEADS, M_DQK).astype(F32) * (M_DQK ** -0.5)
    km = mk.reshape(b, t, M_HEADS, M_DQK).astype(F32)
    vm = mv.reshape(b, t, M_HEADS, M_DV).astype(F32)
    gates = mg.reshape(b, t, 4, M_HEADS).astype(F32) + lp['b_mgate'].astype(F32)
    hm, cs, ns, ms = [], [], [], []
    for d in range(2):
        seqs = (qm, km, vm, gates[:, :, 2 * d], jax.nn.log_sigmoid(gates[:, :, 2 * d + 1]))
        if d == 1:
            seqs = tuple(jnp.flip(z, axis=1) for z in seqs)
        h_d, c_d, n_d, m_d = mlstm_scan(*seqs, init['C'][:, d].astype(F32),
                                        init['n'][:, d].astype(F32), init['m'][:, d].astype(F32))
        hm.append(jnp.flip(h_d, axis=1) if d == 1 else h_d)
        cs.append(c_d)
        ns.append(n_d)
        ms.append(m_d)
    out_m = rmsnorm(hm[0] + hm[1], lp['g_mnorm'].reshape(M_HEADS, M_DV)).reshape(b, t, M_V_W)
    out_m = out_m * jax.nn.sigmoid(mo.astype(F32))

    rr, kr, vr, wdf, wdb, adf, adb, gd = _split(centred_shift_mix(rp, lp['mu']).astype(F32), R_SIZES)

    def heads(z):
        return z.reshape(b, t, R_HEADS, R_HD)

    kk = heads(kr * lp['k_k'])
    kk = kk / jnp.maximum(jnp.sqrt(jnp.sum(kk * kk, axis=-1, keepdims=True)), 1e-12)
    rh = heads(rr)
    vh = heads(vr)
    ys, bon, ss = [], [], []
    for d, (wd, ad) in enumerate(((wdf, adf), (wdb, adb))):
        w_pre = lp['w0'][d] + jnp.tanh(wd) @ lp['w2'][d]
        decay = jnp.exp(-jnp.exp(-jax.nn.softplus(-w_pre) - 0.5))
        a = jax.nn.sigmoid(lp['a0'][d] + ad @ lp['a2'][d])
        k_d = heads(kr * (1.0 + (a - 1.0) * lp['k_a']))
        seqs = (rh, heads(decay), k_d, vh, -kk, kk * heads(a))
        if d == 1:
            seqs = tuple(jnp.flip(z, axis=1) for z in seqs)
        y_d, s_d = wkv7_scan(*seqs, init['S'][:, d].astype(F32))
        ys.append(jnp.flip(y_d, axis=1) if d == 1 else y_d)
        ss.append(s_d)
        bon.append(jnp.sum(rh * k_d * lp['r_k'], axis=-1, keepdims=True) * vh)
    y = ys[0] + ys[1]
    mean = jnp.mean(y, axis=-1, keepdims=True)
    var = jnp.mean(jnp.square(y - mean), axis=-1, keepdims=True)
    yn = (y - mean) * lax.rsqrt(var + GN_EPS) * lp['ln_x'][0].reshape(R_HEADS, R_HD) + lp['ln_x'][1].reshape(R_HEADS, R_HD)
    out_r = (yn + bon[0] + bon[1]).reshape(b, t, R_W) * (jax.nn.sigmoid(gd) @ lp['g2'])

    mixed = jnp.concatenate([att, out_m.astype(h.dtype), out_r.astype(h.dtype)], axis=-1) @ lp['w_out']
    new_ctx = dict(k=k, v=v, C=jnp.stack(cs, axis=1), n=jnp.stack(ns, axis=1),
                   m=jnp.stack(ms, axis=1), S=jnp.stack(ss, axis=1))
    return mixed, new_ctx


def trunk_layer(x, cond, lp, rope, ctx_kv, init):
    mod = jax.nn.silu(cond.astype(F32)) @ lp['w_mod'] + lp['b_mod']
    mod = mod.reshape(cond.shape[0], 1, N_MOD, D_MODEL).astype(x.dtype)
    shift1, scale1, gate1 = mod[:, :, 0], mod[:, :, 1], mod[:, :, 2]
    shift2, scale2, gate2 = mod[:, :, 3], mod[:, :, 4], mod[:, :, 5]
    h = rmsnorm(x, lp['g_norm'][0]) * (1.0 + scale1) + shift1
    mixed, new_ctx = mixers(h, lp, rope, ctx_kv, init)
    x = x + gate1 * rmsnorm(mixed, lp['g_norm'][1])
    h = rmsnorm(x, lp['g_norm'][2]) * (1.0 + scale2) + shift2
    f = (jax.nn.silu(h @ lp['w_gate']) * (h @ lp['w_up'])) @ lp['w_down']
    x = x + gate2 * rmsnorm(f, lp['g_norm'][3])
    return x, new_ctx


def setup_inputs(seed: int = 0) -> dict:
    key = jax.random.key(seed)
    keys = jax.random.split(key, 40)
    ctr = [0]

    def nxt():
        ctr[0] += 1
        return keys[ctr[0] - 1]

    def nrm(shape, scale=1.0):
        return scale * jax.random.normal(nxt(), shape, jnp.float32)

    def uni(shape, lo, hi):
        return jax.random.uniform(nxt(), shape, jnp.float32, lo, hi)

    L = DEPTH
    return {
        'x_prompt': nrm((BATCH, SEQ, D_MODEL)),
        'x_sample': nrm((DEC_BATCH, DEC_SEQ, D_MODEL)),
        'cache_attn_k': nrm((DEC_BATCH, L, PAST_LEN, ATT_KV_HEADS, HEAD_DIM)),
        'cache_attn_v': nrm((DEC_BATCH, L, PAST_LEN, ATT_KV_HEADS, HEAD_DIM)),
        'state_mlstm_C': nrm((DEC_BATCH, L, 2, M_HEADS, M_DQK, M_DV), 0.5),
        'state_mlstm_n': nrm((DEC_BATCH, L, 2, M_HEADS, M_DQK), 0.5),
        'state_mlstm_m': nrm((DEC_BATCH, L, 2, M_HEADS), 1.0),
        'state_rwkv': nrm((DEC_BATCH, L, 2, R_HEADS, R_HD, R_HD), 0.5),
        'c': nrm((DEC_BATCH, D_MODEL)),
        'c_ctx': nrm((D_MODEL,)),
        'w_mod': nrm((L, D_MODEL, N_MOD * D_MODEL), 0.5 * D_MODEL ** -0.5),
        'b_mod': nrm((L, N_MOD * D_MODEL), 0.01),
        'g_norm': 1.0 + nrm((L, 4, D_MODEL), 0.02),
        'w_in': nrm((L, D_MODEL, IN_COLS), D_MODEL ** -0.5),
        'g_qk': 1.0 + nrm((L, 2, HEAD_DIM), 0.02),
        'b_mgate': jnp.array([-1.0, 4.0, -1.0, 4.0], jnp.float32)[None, :, None] + nrm((L, 4, M_HEADS), 0.5),
        'g_mnorm': 1.0 + nrm((L, M_V_W), 0.02),
        'mu_rwkv': uni((L, R_COLS), 0.0, 1.0),
        'w0_rwkv': uni((L, 2, R_W), -6.0, 1.0),
        'w2_rwkv': nrm((L, 2, DECAY_LORA, R_W), DECAY_LORA ** -0.5),
        'a0_rwkv': nrm((L, 2, R_W), 0.1),
        'a2_rwkv': nrm((L, 2, AAA_LORA, R_W), AAA_LORA ** -0.5),
        'g2_rwkv': nrm((L, GATE_LORA, R_W), GATE_LORA ** -0.5),
        'k_k_rwkv': 0.85 + nrm((L, R_W), 0.02),
        'k_a_rwkv': 1.0 + nrm((L, R_W), 0.02),
        'r_k_rwkv': nrm((L, R_HEADS, R_HD), 0.1),
        'ln_x_rwkv': jnp.stack([1.0 + nrm((L, R_W), 0.02), nrm((L, R_W), 0.01)], axis=1),
        'w_out': nrm((L, MIX_W, D_MODEL), MIX_W ** -0.5),
        'w_gate': nrm((L, D_MODEL, D_FF), D_MODEL ** -0.5),
        'w_up': nrm((L, D_MODEL, D_FF), D_MODEL ** -0.5),
        'w_down': nrm((L, D_FF, D_MODEL), D_FF ** -0.5),
    }


def reference(x_prompt, x_sample, cache_attn_k, cache_attn_v, state_mlstm_C, state_mlstm_n,
              state_mlstm_m, state_rwkv, c, c_ctx, w_mod, b_mod, g_norm, w_in, g_qk, b_mgate,
              g_mnorm, mu_rwkv, w0_rwkv, w2_rwkv, a0_rwkv, a2_rwkv, g2_rwkv, k_k_rwkv, k_a_rwkv,
              r_k_rwkv, ln_x_rwkv, w_out, w_gate, w_up, w_down):
    def layer_params(l):
        return dict(w_mod=w_mod[l], b_mod=b_mod[l], g_norm=g_norm[l], w_in=w_in[l], g_qk=g_qk[l],
                    b_mgate=b_mgate[l], g_mnorm=g_mnorm[l], mu=mu_rwkv[l], w0=w0_rwkv[l],
                    w2=w2_rwkv[l], a0=a0_rwkv[l], a2=a2_rwkv[l], g2=g2_rwkv[l], k_k=k_k_rwkv[l],
                    k_a=k_a_rwkv[l], r_k=r_k_rwkv[l], ln_x=ln_x_rwkv[l], w_out=w_out[l],
                    w_gate=w_gate[l], w_up=w_up[l], w_down=w_down[l])

    bp = x_prompt.shape[0]
    zero_init = dict(C=jnp.zeros((bp, 2, M_HEADS, M_DQK, M_DV), F32),
                     n=jnp.zeros((bp, 2, M_HEADS, M_DQK), F32),
                     m=jnp.zeros((bp, 2, M_HEADS), F32),
                     S=jnp.zeros((bp, 2, R_HEADS, R_HD, R_HD), F32))
    xp = x_prompt
    ks, vs, cs, ns, ms, ss = [], [], [], [], [], []
    for l in range(DEPTH):
        xp, st = trunk_layer(xp, c_ctx[None, :], layer_params(l), None, None, zero_init)
        ks.append(st['k'])
        vs.append(st['v'])
        cs.append(st['C'])
        ns.append(st['n'])
        ms.append(st['m'])
        ss.append(st['S'])

    rows = x_sample.shape[1] // GRID_W
    rope = axial_rope(rows)
    xs = x_sample
    for l in range(DEPTH):
        init = dict(C=state_mlstm_C[:, l], n=state_mlstm_n[:, l], m=state_mlstm_m[:, l], S=state_rwkv[:, l])
        xs, _ = trunk_layer(xs, c, layer_params(l), rope, (cache_attn_k[:, l], cache_attn_v[:, l]), init)

    return (xp, xs, jnp.stack(ks, axis=1), jnp.stack(vs, axis=1), jnp.stack(cs, axis=1),
            jnp.stack(ns, axis=1), jnp.stack(ms, axis=1), jnp.stack(ss, axis=1))
```

```python
import numpy as np
import concourse.bass as bass
import concourse.mybir as mybir
from concourse.bass_utils import run_bass_kernel_spmd
from contextlib import ExitStack

F32 = mybir.dt.float32
BF16 = mybir.dt.bfloat16
ALU = mybir.AluOpType
AF = mybir.ActivationFunctionType
AX = mybir.AxisListType

ENGS = ("pe", "act", "dve", "pool", "sp")


class _Op:
    __slots__ = ("eng", "fn", "deps", "needs_inc", "semval", "is_dma", "dsem", "dval")

    def __init__(self, eng, fn, is_dma, dsem):
        self.eng = eng
        self.fn = fn
        self.deps = set()
        self.needs_inc = False
        self.semval = 0
        self.is_dma = is_dma
        self.dsem = dsem
        self.dval = 0


class _St:
    __slots__ = ("w", "r")

    def __init__(self):
        self.w = None
        self.r = []


class Prog:
    def __init__(self, nc, stack):
        self.nc = nc
        self.ops = []
        self.keys = {}
        self.esem = {e: stack.enter_context(nc.semaphore("es_" + e)) for e in ENGS if e != "sp"}
        self.ecount = {e: 0 for e in self.esem}
        self.dsems = {}
        self.dcount = {}
        self.stack = stack
        self.waited = {e: {} for e in ENGS}
        self.barrier = {}
        self.nphase = 0
        self.ninstr = 0

    def _dsem(self, name, eng="sp"):
        cls = "sw" if eng == "pool" else "hw"
        name = cls + ":" + name
        pm = self.__dict__.setdefault("phase_map", {})
        if name not in pm:
            j = sum(1 for k in pm if k.startswith(cls + ":"))
            phys = "%s%d" % (cls, j)
            if phys not in self.dsems:
                self.dsems[phys] = self.stack.enter_context(self.nc.semaphore("ds_" + phys))
                self.dcount[phys] = 0
            pm[name] = phys
        return pm[name]

    def op(self, eng, fn, reads=(), writes=(), dsem=None):
        is_dma = dsem is not None
        o = _Op(eng, fn, is_dma, dsem)
        if is_dma:
            dsem = self._dsem(dsem, eng)
            o.dsem = dsem
            self.dcount[dsem] += 1
            o.dval = 16 * self.dcount[dsem]
        deps = set()
        for k in reads:
            st = self.keys.get(k)
            if st is not None and st.w is not None:
                deps.add(st.w)
        for k in writes:
            st = self.keys.get(k)
            if st is not None:
                if st.w is not None:
                    deps.add(st.w)
                deps.update(st.r)
        deps.discard(o)
        for k in reads:
            self.keys.setdefault(k, _St()).r.append(o)
        for k in writes:
            st = self.keys.setdefault(k, _St())
            st.w = o
            st.r = []
        o.deps = deps
        for d in deps:
            if not d.is_dma:
                d.needs_inc = True
        self.ops.append(o)
        self.ninstr += 1
        return o

    def flush(self):
        nc = self.nc
        per = {e: [] for e in ENGS}
        for o in self.ops:
            per[o.eng].append(o)
        for e in self.esem:
            for o in reversed(per[e]):
                if not o.is_dma:
                    o.needs_inc = True
                    break
        for e in self.esem:
            c = self.ecount[e]
            for o in per[e]:
                if o.is_dma:
                    continue
                if o.needs_inc:
                    c += 1
                o.semval = c
            self.ecount[e] = c
        barrier = dict(self.barrier)
        esem, dsems, waited = self.esem, self.dsems, self.waited

        def emit(e, eng):
            first = True
            for o in per[e]:
                targets = {}
                if first:
                    for nm, (h, v) in barrier.items():
                        targets[nm] = (h, v)
                    first = False
                for d in o.deps:
                    if d.is_dma:
                        nm, h, v = "d_" + d.dsem, dsems[d.dsem], d.dval
                    else:
                        if d.eng == "pe" and e == "pe":
                            continue
                        nm, h, v = "e_" + d.eng, esem[d.eng], d.semval
                    if nm not in targets or targets[nm][1] < v:
                        targets[nm] = (h, v)
                for nm, (h, v) in targets.items():
                    if v <= 0 or waited[e].get(nm, 0) >= v:
                        continue
                    eng.wait_ge(h, v)
                    waited[e][nm] = v
                ins = o.fn(eng)
                if o.is_dma:
                    ins.then_inc(dsems[o.dsem], 16)
                elif o.needs_inc:
                    ins.then_inc(esem[e], 1)

        with nc.Block() as block:
            if per["pe"]:
                block.tensor(lambda eng: emit("pe", eng))
            if per["act"]:
                block.scalar(lambda eng: emit("act", eng))
            if per["dve"]:
                block.vector(lambda eng: emit("dve", eng))
            if per["pool"]:
                block.gpsimd(lambda eng: emit("pool", eng))
            if per["sp"]:
                block.sync(lambda eng: emit("sp", eng))
        self.barrier = {}
        for e in self.esem:
            if self.ecount[e] > 0:
                self.barrier["e_" + e] = (self.esem[e], self.ecount[e])
        for nm, h in self.dsems.items():
            if self.dcount[nm] > 0:
                self.barrier["d_" + nm] = (h, 16 * self.dcount[nm])
        self.ops = []
        self.keys = {}
        self.phase_map = {}
        self.nphase += 1

    def final_wait(self):
        nc = self.nc
        barrier = dict(self.barrier)
        with nc.Block() as block:
            def f(eng):
                for nm, (h, v) in barrier.items():
                    eng.wait_ge(h, v)
            block.sync(f)


def _is_ap(v):
    return hasattr(v, "tensor") and hasattr(v, "partition_size")


class KAP:
    def __init__(self, ap, key):
        self.ap = ap
        self.key = key

    def __getitem__(self, idx):
        return KAP(self.ap[idx], self.key)


class Eng:
    def __init__(self, P, name):
        self.P = P
        self.name = name

    def __getattr__(self, opname):
        P, ename = self.P, self.name

        def call(**kw):
            reads, writes = [], []
            sb_out = sb_in = None
            for k, v in list(kw.items()):
                okey = None
                if isinstance(v, KAP):
                    okey = v.key
                    v = v.ap
                    kw[k] = v
                if _is_ap(v):
                    if type(v.tensor).__name__ == "DRamTensorHandle":
                        continue
                    key = okey or v.tensor.name
                    if k in ("out", "accum_out", "ap"):
                        writes.append(key)
                        if k == "out":
                            sb_out = key
                    else:
                        reads.append(key)
                        if k == "in_":
                            sb_in = key
            dsem = None
            if opname == "dma_start":
                dsem = kw.pop("dsem", None)
                if dsem is None:
                    nm = sb_out if sb_out is not None else sb_in
                    nm = nm.rsplit("_", 1)[0] if nm is not None else "dd"
                    dsem = ("L_" if sb_out is not None else "S_") + nm
            if opname == "matmul" and not kw.get("start", True):
                reads.append(writes[0])
            if opname == "memset":
                a, c = kw["ap"], kw["constant"]
                return P.op(ename, lambda e: e.memset(a, c), reads, writes)
            return P.op(ename, lambda e: getattr(e, opname)(**kw), reads, writes, dsem=dsem)

        return call


class Ctx:
    def __init__(self, nc, stack):
        self.nc = nc
        self.P = Prog(nc, stack)
        self.pe = Eng(self.P, "pe")
        self.act = Eng(self.P, "act")
        self.dve = Eng(self.P, "dve")
        self.pool = Eng(self.P, "pool")
        self.sp = Eng(self.P, "sp")
        self.uid = 0

    def sb(self, st, name, shape, dt=F32):
        self.uid += 1
        return st.enter_context(self.nc.sbuf_tensor("%s_%d" % (name, self.uid), list(shape), dt))

    def ps(self, st, name, shape, dt=F32):
        self.uid += 1
        return st.enter_context(self.nc.psum_tensor("%s_%d" % (name, self.uid), list(shape), dt))

    def flush(self):
        self.P.flush()


class Cfg:
    def __init__(self, **kw):
        self.D = 4096
        self.NSEQ = 4
        self.TP = 256
        self.TS = 1024
        self.PAST = 256
        self.DEPTH = 2
        self.AH = 16
        self.KVH = 4
        self.HD = 128
        self.MH = 4
        self.MDQK = 128
        self.MDV = 256
        self.RH = 16
        self.RHD = 64
        self.DFF = 11008
        self.GRID_W = 64
        for k, v in kw.items():
            setattr(self, k, v)
        c = self
        c.ATT_W = c.AH * c.HD
        c.KV_W = c.KVH * c.HD
        c.MQK_W = c.MH * c.MDQK
        c.MV_W = c.MH * c.MDV
        c.MG_W = 4 * c.MH
        c.R_W = c.RH * c.RHD
        c.RL_W = 384
        c.R_COLS = 3 * c.R_W + c.RL_W
        c.MIX_W = c.ATT_W + c.MV_W + c.R_W
        sizes = (c.ATT_W, c.KV_W, c.KV_W, c.MQK_W, c.MQK_W, c.MV_W, c.MV_W, c.MG_W, c.R_COLS)
        offs = np.concatenate([[0], np.cumsum(sizes)])
        (c.O_AQ, c.O_AK, c.O_AV, c.O_MQ, c.O_MK, c.O_MV, c.O_MO, c.O_MG, c.O_RP) = [int(x) for x in offs[:-1]]
        c.IN_COLS = int(offs[-1])
        c.NTOK = c.NSEQ * c.TP + c.TS
        c.groups = [(0, c.NSEQ * c.TP, 0), (c.NSEQ * c.TP, c.TS, 1)]
        c.seqs = [(i * c.TP, c.TP, 0, i) for i in range(c.NSEQ)] + [(c.NSEQ * c.TP, c.TS, 1, 0)]


def cdiv(a, b):
    return (a + b - 1) // b


def gemm(X, hT, KT, toks, jobs, NTILE=512, KSPLIT=1):
    with ExitStack() as st:
        KC = cdiv(KT, KSPLIT)
        NQ = 4 if KC >= 8 else 1
        KQ = cdiv(KC, NQ)
        wb = [[X.sb(st, "wb%d_%d" % (b, q), [128, KQ, NTILE], BF16) for q in range(NQ)] for b in range(2)]
        nps = len(toks) if KSPLIT > 1 else 2
        ps = [X.ps(st, "gps%d" % i, [128, NTILE]) for i in range(nps)]
        cnt = 0
        wcnt = 0
        for (W_ap, n0, nsz, evac) in jobs:
            Wv = W_ap.rearrange("(kt p) n -> p kt n", p=128)
            for kc in range(KSPLIT):
                k0 = kc * KC
                k1 = min(KT, k0 + KC)
                b = wcnt % 2
                wcnt += 1
                for q in range(NQ):
                    a0 = k0 + q * KQ
                    a1 = min(k1, a0 + KQ)
                    if a1 <= a0:
                        continue
                    X.pool.dma_start(out=wb[b][q][:, 0:a1 - a0, 0:nsz], in_=Wv[:, a0:a1, n0:n0 + nsz])
                for ti, (t0, tsz) in enumerate(toks):
                    if KSPLIT > 1:
                        pt = ps[ti]
                    else:
                        pt = ps[cnt % 2]
                        cnt += 1
                    for kt in range(k0, k1):
                        q, r = divmod(kt - k0, KQ)
                        X.pe.matmul(out=pt[0:tsz, 0:nsz], lhsT=hT[:, kt, t0:t0 + tsz], rhs=wb[b][q][:, r, 0:nsz],
                                    start=(kt == 0), stop=(kt == KT - 1))
                    if kc == KSPLIT - 1:
                        evac(ti, n0, nsz, pt)
        X.flush()


def wjobs(W_ap, N, evac, NTILE=512):
    return [(W_ap, j * NTILE, min(NTILE, N - j * NTILE), evac) for j in range(cdiv(N, NTILE))]


def transpose_into(X, ps_tr, src_bf, ncols, hT, col0, ident_bf, alt=[0]):
    KT = cdiv(ncols, 128)
    g = 0
    while g < KT:
        ng = min(8, KT - g)
        pt = ps_tr[alt[0] % 2]
        for i in range(ng):
            kt = g + i
            X.pe.transpose(out=pt[:, i, :], in_=src_bf[:, kt * 128:(kt + 1) * 128], identity=ident_bf[:])
        if alt[0] % 2 == 0:
            X.dve.tensor_copy(out=hT[:, g:g + ng, col0:col0 + 128], in_=pt[:, 0:ng, :])
        else:
            X.act.activation(out=hT[:, g:g + ng, col0:col0 + 128], in_=pt[:, 0:ng, :], func=AF.Copy)
        alt[0] += 1
        g += ng


def rsqrt_ip(X, ap):
    X.act.activation(out=ap, in_=ap, func=AF.Sqrt)
    X.dve.reciprocal(out=ap, in_=ap)


def rms_rstd(X, st, xt, n, eps, junk, name="rs"):
    ss = X.sb(st, name + "ss", [128, 1])
    rs = X.sb(st, name + "r", [128, 1])
    X.act.activation(out=junk, in_=xt, func=AF.Square, accum_out=ss[:])
    X.dve.tensor_scalar(out=rs[:], in0=ss[:], scalar1=1.0 / n, scalar2=eps, op0=ALU.mult, op1=ALU.add)
    rsqrt_ip(X, rs[:])
    return rs


def make_hT(X, C, K, src_ap, ntok, hT, norm=None):
    ncols = src_ap.shape[1]
    with ExitStack() as st:
        ps_tr = [X.ps(st, "ptr%d" % i, [128, 8, 128], BF16) for i in range(2)]
        hb = [X.sb(st, "hb%d" % i, [128, ncols], BF16) for i in range(2)]
        if norm is not None:
            xs = [X.sb(st, "xs%d" % i, [128, ncols]) for i in range(2)]
            tmp = X.sb(st, "ntmp", [128, ncols])
            junk = X.sb(st, "njunk", [128, ncols], BF16)
        for i in range(ntok // 128):
            b = i % 2
            if norm is None:
                X.pool.dma_start(out=hb[b][:], in_=src_ap[i * 128:(i + 1) * 128, :])
            else:
                G, SH = norm
                X.sp.dma_start(out=xs[b][:], in_=src_ap[i * 128:(i + 1) * 128, :])
                rs = rms_rstd(X, st, xs[b][:], ncols, 1e-6, junk[:], name="n%d" % i)
                X.dve.scalar_tensor_tensor(out=tmp[:], in0=xs[b][:], scalar=rs[:, 0:1], in1=G[:], op0=ALU.mult, op1=ALU.mult)
                X.pool.tensor_tensor(out=hb[b][:], in0=tmp[:], in1=SH[:], op=ALU.add)
            transpose_into(X, ps_tr, hb[b], ncols, hT, i * 128, K["ident_bf"])
        X.flush()


def load_rows(X, dst, vec_ap):
    X.sp.dma_start(out=dst, in_=vec_ap.partition_broadcast(128))


def store_evac(X, st, dst_ap, r0, name="ev", engs=("act", "dve")):
    bufs = [X.sb(st, name + "%d" % i, [128, 512]) for i in range(4)]
    cnt = [0]

    def evac(ti, n0, nsz, pt, tsz=128):
        b = bufs[cnt[0] % 4]
        if cnt[0] % 2 == 0:
            X.act.activation(out=b[0:tsz, 0:nsz], in_=pt[0:tsz, 0:nsz], func=AF.Copy)
        else:
            X.dve.tensor_copy(out=b[0:tsz, 0:nsz], in_=pt[0:tsz, 0:nsz])
        cnt[0] += 1
        X.sp.dma_start(out=dst_ap[r0 + ti * 128:r0 + ti * 128 + tsz, n0:n0 + nsz], in_=b[0:tsz, 0:nsz])
    return evac


def stage_mod(X, C, K, A, l):
    KT = C.D // 128
    with ExitStack() as st:
        cf = X.sb(st, "cf", [128, KT, 2])
        scT = X.sb(st, "scT", [128, KT, 2], BF16)
        bm = X.sb(st, "bm", [2, 6 * C.D])
        for c in range(2):
            X.sp.dma_start(out=cf[:, :, c:c + 1], in_=A["cond2"][c:c + 1, :].rearrange("c (kt p) -> p kt c", p=128),
                           allow_slow_non_contiguous=True, dsem="L_cf%d" % c)
        X.act.activation(out=scT[:], in_=cf[:], func=AF.Silu)
        X.sp.dma_start(out=bm[:], in_=A["b_mod"][l, :].partition_broadcast(2))
        ob = [X.sb(st, "mo%d" % i, [2, 512]) for i in range(2)]
        cnt = [0]

        def evac(ti, n0, nsz, pt):
            b = ob[cnt[0] % 2]
            cnt[0] += 1
            X.dve.tensor_tensor(out=b[:, 0:nsz], in0=pt[0:2, 0:nsz], in1=bm[:, n0:n0 + nsz], op=ALU.add)
            X.sp.dma_start(out=A["MOD"][:, n0:n0 + nsz], in_=b[:, 0:nsz])
        gemm(X, scT, KT, [(0, 2)], wjobs(A["w_mod"][l], 6 * C.D, evac))


def mod_tiles(X, C, A, st, l, cond, which, kind):
    D = C.D
    o = 3 * which
    if kind == "norm":
        G = X.sb(st, "G", [128, D])
        SH = X.sb(st, "SH", [128, D])
    else:
        GG = X.sb(st, "GG", [128, D])
    with ExitStack() as tmp:
        t = X.sb(tmp, "mt", [128, D])
        if kind == "norm":
            load_rows(X, t[:], A["MOD"][cond, (o + 1) * D:(o + 2) * D])
            load_rows(X, G[:], A["g_norm"][l, 2 * which, :])
            X.dve.scalar_tensor_tensor(out=G[:], in0=t[:], scalar=1.0, in1=G[:], op0=ALU.add, op1=ALU.mult)
            load_rows(X, SH[:], A["MOD"][cond, o * D:(o + 1) * D])
            res = (G, SH)
        else:
            load_rows(X, GG[:], A["MOD"][cond, (o + 2) * D:(o + 3) * D])
            load_rows(X, t[:], A["g_norm"][l, 2 * which + 1, :])
            X.dve.tensor_tensor(out=GG[:], in0=GG[:], in1=t[:], op=ALU.mult)
            res = GG
        X.flush()
    return res


def resid_pass(X, C, raw_ap, xold_ap, xnew_ap, tok0, ntok, GG):
    D = C.D
    with ExitStack() as st:
        rw = [X.sb(st, "rw%d" % i, [128, D]) for i in range(2)]
        xo = [X.sb(st, "xo%d" % i, [128, D]) for i in range(2)]
        junk = X.sb(st, "rjunk", [128, D], BF16)
        for i in range(ntok // 128):
            b = i % 2
            r0 = tok0 + i * 128
            X.sp.dma_start(out=rw[b][:], in_=raw_ap[r0:r0 + 128, :])
            X.sp.dma_start(out=xo[b][:], in_=xold_ap[r0:r0 + 128, :])
            rs = rms_rstd(X, st, rw[b][:], D, 1e-6, junk[:], name="q%d" % i)
            X.dve.scalar_tensor_tensor(out=rw[b][:], in0=rw[b][:], scalar=rs[:, 0:1], in1=GG[:], op0=ALU.mult, op1=ALU.mult)
            X.pool.tensor_tensor(out=xo[b][:], in0=xo[b][:], in1=rw[b][:], op=ALU.add)
            X.act.dma_start(out=xnew_ap[r0:r0 + 128, :], in_=xo[b][:])
        X.flush()


def layer(X, C, K, A, l, x_in, x_out, mixers_fn):
    D = C.D
    KT = D // 128
    stage_mod(X, C, K, A, l)
    for (tok0, ntok, cond) in C.groups:
        with ExitStack() as st:
            hT = X.sb(st, "hT", [128, KT, ntok], BF16)
            with ExitStack() as st2:
                G, SH = mod_tiles(X, C, A, st2, l, cond, 0, "norm")
                make_hT(X, C, K, x_in[tok0:tok0 + ntok, :], ntok, hT, norm=(G, SH))
            with ExitStack() as st2:
                ev = store_evac(X, st2, A["PROJ"], tok0)
                gemm(X, hT, KT, [(i * 128, 128) for i in range(ntok // 128)], wjobs(A["w_in"][l], C.IN_COLS, ev))
    mixers_fn(X, C, K, A, l)
    KTM = C.MIX_W // 128
    for (tok0, ntok, cond) in C.groups:
        with ExitStack() as st:
            hT = X.sb(st, "mT", [128, KTM, ntok], BF16)
            make_hT(X, C, K, A["MIX"][tok0:tok0 + ntok, :], ntok, hT)
            with ExitStack() as st2:
                ev = store_evac(X, st2, A["RAW"], tok0)
                gemm(X, hT, KTM, [(i * 128, 128) for i in range(ntok // 128)], wjobs(A["w_out"][l], D, ev))
        with ExitStack() as st:
            GG = mod_tiles(X, C, A, st, l, cond, 0, "gate")
            resid_pass(X, C, A["RAW"], x_in, A["X1"], tok0, ntok, GG)
    for (tok0, ntok, cond) in C.groups:
        with ExitStack() as st:
            hT = X.sb(st, "h2T", [128, KT, ntok], BF16)
            with ExitStack() as st2:
                G, SH = mod_tiles(X, C, A, st2, l, cond, 1, "norm")
                make_hT(X, C, K, A["X1"][tok0:tok0 + ntok, :], ntok, hT, norm=(G, SH))
            nt = ntok // 128
            with ExitStack() as st2:
                sg = X.sb(st2, "sg", [128, nt, 512])
                hb = [X.sb(st2, "hid%d" % i, [128, 512]) for i in range(2)]
                cnt = [0]

                def ev_gate(ti, n0, nsz, pt):
                    X.act.activation(out=sg[:, ti, 0:nsz], in_=pt[:, 0:nsz], func=AF.Silu)

                def ev_up(ti, n0, nsz, pt):
                    b = hb[cnt[0] % 2]
                    cnt[0] += 1
                    X.dve.tensor_tensor(out=b[:, 0:nsz], in0=pt[:, 0:nsz], in1=sg[:, ti, 0:nsz], op=ALU.mult)
                    X.sp.dma_start(out=A["HID"][tok0 + ti * 128:tok0 + (ti + 1) * 128, n0:n0 + nsz], in_=b[:, 0:nsz])
                toks = [(i * 128, 128) for i in range(nt)]
                jobs = []
                for j in range(cdiv(C.DFF, 512)):
                    n0 = j * 512
                    nsz = min(512, C.DFF - n0)
                    jobs.append((A["w_gate"][l], n0, nsz, ev_gate))
                    jobs.append((A["w_up"][l], n0, nsz, ev_up))
                gemm(X, hT, KT, toks, jobs)
    KTF = C.DFF // 128
    for (tok0, ntok, cond) in C.groups:
        TG = min(512, ntok)
        for s0 in range(0, ntok, TG):
            with ExitStack() as st:
                hT = X.sb(st, "fT", [128, KTF, TG], BF16)
                make_hT(X, C, K, A["HID"][tok0 + s0:tok0 + s0 + TG, :], TG, hT)
                with ExitStack() as st2:
                    ev = store_evac(X, st2, A["RAW"], tok0 + s0)
                    gemm(X, hT, KTF, [(i * 128, 128) for i in range(TG // 128)], wjobs(A["w_down"][l], D, ev, NTILE=256),
                         NTILE=256, KSPLIT=2)
        with ExitStack() as st:
            GG = mod_tiles(X, C, A, st, l, cond, 1, "gate")
            resid_pass(X, C, A["RAW"], A["X1"], x_out, tok0, ntok, GG)


W_NAMES = ["w_mod", "b_mod", "g_norm", "w_in", "g_qk", "b_mgate", "g_mnorm", "mu_rwkv", "w0_rwkv", "w2_rwkv",
           "a0_rwkv", "a2_rwkv", "g2_rwkv", "k_k_rwkv", "k_a_rwkv", "r_k_rwkv", "ln_x_rwkv", "w_out", "w_gate",
           "w_up", "w_down"]


def input_shapes(C):
    L, D = C.DEPTH, C.D
    return {
        "x_in": [C.NTOK, D], "cond2": [2, D],
        "cache_k": [L, C.PAST, C.KV_W], "cache_v": [L, C.PAST, C.KV_W],
        "st_C": [L, 2, C.MH, C.MDQK, C.MDV], "st_n": [L, 2, C.MH, C.MDQK], "st_m": [L, 2, C.MH],
        "st_S": [L, 2, C.RH, C.RHD, C.RHD],
        "w_mod": [L, D, 6 * D], "b_mod": [L, 6 * D], "g_norm": [L, 4, D], "w_in": [L, D, C.IN_COLS],
        "g_qk": [L, 2, C.HD], "b_mgate": [L, 4 * C.MH], "g_mnorm": [L, C.MV_W], "mu_rwkv": [L, C.R_COLS],
        "w0_rwkv": [L, 2, C.R_W], "w2_rwkv": [L, 2, 64, C.R_W], "a0_rwkv": [L, 2, C.R_W],
        "a2_rwkv": [L, 2, 64, C.R_W], "g2_rwkv": [L, 128, C.R_W], "k_k_rwkv": [L, C.R_W], "k_a_rwkv": [L, C.R_W],
        "r_k_rwkv": [L, C.R_W], "ln_x_rwkv": [L, 2, C.R_W], "w_out": [L, C.MIX_W, D], "w_gate": [L, D, C.DFF],
        "w_up": [L, D, C.DFF], "w_down": [L, C.DFF, D],
        "c_ident": [128, 128], "c_masks": [NMASK, 128, 128], "c_cos": [C.TS, 64], "c_sin": [C.TS, 64],
    }


def output_shapes(C):
    L = C.DEPTH
    return {
        "y": [C.NTOK, C.D],
        "o_k": [C.NSEQ, L, C.TP, C.KV_W], "o_v": [C.NSEQ, L, C.TP, C.KV_W],
        "o_C": [C.NSEQ, L, 2, C.MH, C.MDQK, C.MDV], "o_n": [C.NSEQ, L, 2, C.MH, C.MDQK],
        "o_m": [C.NSEQ, L, 2, C.MH], "o_S": [C.NSEQ, L, 2, C.RH, C.RHD, C.RHD],
    }


def scratch_shapes(C):
    rw = {k: [C.NTOK, C.R_W] for k in ("RR", "RV", "RKK", "RLW0", "RLW1", "RKD0", "RKD1", "RB0", "RB1", "RBON", "RGATE",
                                       "RY0", "RY1")}
    return {**rw, "MOD": [2, 6 * C.D], "PROJ": [C.NTOK, C.IN_COLS], "MIX": [C.NTOK, C.MIX_W], "RAW": [C.NTOK, C.D],
            "X1": [C.NTOK, C.D], "X2": [C.NTOK, C.D], "HID": [C.NTOK, C.DFF]}


M_INCL, M_STRICT, M_NEG, M_NEGT, M_LVL, M_ISI, M_L0T = 0, 2, 4, 6, 8, 22, 28
NMASK = 30


def make_consts(C):
    s = np.arange(128)[:, None]
    t = np.arange(128)[None, :]
    masks = np.zeros((NMASK, 128, 128), np.float32)
    for d in range(2):
        before = (s <= t) if d == 0 else (s >= t)
        strict = (s < t) if d == 0 else (s > t)
        masks[M_INCL + d] = before
        masks[M_STRICT + d] = strict
        masks[M_NEGT + d] = np.where(before, 0.0, -1e30)
        masks[M_NEG + d] = np.where(before.T, 0.0, -1e30)
        for lv in range(7):
            b = 1 << lv
            same = (s // (2 * b)) == (t // (2 * b))
            if d == 0:
                m = same & (s % (2 * b) < b) & (t % (2 * b) >= b)
            else:
                m = same & (s % (2 * b) >= b) & (t % (2 * b) < b)
            masks[M_LVL + d * 7 + lv] = m
            if lv == 0:
                masks[M_L0T + d] = m.T
        masks[M_ISI + 3 * d + 0] = before
        masks[M_ISI + 3 * d + 1] = strict
        masks[M_ISI + 3 * d + 2] = before
    rows = C.TS // C.GRID_W
    row = np.repeat(np.arange(rows), C.GRID_W).astype(np.float32)
    col = np.tile(np.arange(C.GRID_W), rows).astype(np.float32)
    inv = (10000.0 ** (-np.arange(0, 64, 2, dtype=np.float32) / 64)).astype(np.float32)
    ang = np.concatenate([row[:, None] * inv, col[:, None] * inv], axis=-1).astype(np.float32)
    return {"c_ident": np.eye(128, dtype=np.float32), "c_masks": masks,
            "c_cos": np.cos(ang).astype(np.float32), "c_sin": np.sin(ang).astype(np.float32)}


def dummy_mixers(X, C, K, A, l):
    X.sp.dma_start(out=A["MIX"][:, :], in_=A["PROJ"][:, 0:C.MIX_W])
    X.flush()


def build(C, mixers_fn=None, debug_outs=()):
    nc = bass.Bass("TRN2", target_bir_lowering=False)
    A = {}
    for k, shp in input_shapes(C).items():
        A[k] = nc.dram_tensor(k, shp, F32, kind="ExternalInput").ap()
    for k, shp in output_shapes(C).items():
        A[k] = nc.dram_tensor(k, shp, F32, kind="ExternalOutput").ap()
    for k, shp in scratch_shapes(C).items():
        kind = "ExternalOutput" if k in debug_outs else "Internal"
        A[k] = nc.dram_tensor(k, shp, F32, kind=kind).ap()
    with ExitStack() as st:
        X = Ctx(nc, st)
        K = {}
        K["ident_f"] = X.sb(st, "identf", [128, 128])
        K["ident_bf"] = X.sb(st, "identb", [128, 128], BF16)
        X.sp.dma_start(out=K["ident_f"][:], in_=A["c_ident"])
        X.dve.tensor_copy(out=K["ident_bf"][:], in_=K["ident_f"][:])
        K["ones_f"] = X.sb(st, "onesf", [128, 128])
        X.dve.memset(ap=K["ones_f"][:], constant=1.0)
        K["masks"] = X.sb(st, "masks", [128, NMASK, 128])
        X.sp.dma_start(out=K["masks"][:], in_=A["c_masks"].rearrange("m p s -> p m s"))
        X.flush()
        mf = mixers_fn or dummy_mixers
        xs = [A["x_in"], A["X2"], A["y"]]
        for l in range(C.DEPTH):
            x_in = A["x_in"] if l == 0 else A["X2"]
            x_out = A["y"] if l == C.DEPTH - 1 else A["X2"]
            layer(X, C, K, A, l, x_in, x_out, mf)
        X.flush()
        X.P.final_wait()
        print("instructions:", X.P.ninstr, "phases:", X.P.nphase, "dma sems:", len(X.P.dsems))
    return nc


def head_rmsnorm(X, st, xt, H, hd, gt, eps, out, name, tmp):
    ss = X.sb(st, name + "ss", [128, H])
    xv = xt[:, 0:H * hd].rearrange("p (h d) -> p h d", h=H)
    tv = tmp[:, 0:H * hd].rearrange("p (h d) -> p h d", h=H)
    ov = out[:, 0:H * hd].rearrange("p (h d) -> p h d", h=H)
    X.dve.tensor_tensor(out=tv, in0=xv, in1=xv, op=ALU.mult)
    X.dve.tensor_reduce(out=ss[:], in_=tv, axis=AX.X, op=ALU.add)
    X.dve.tensor_scalar(out=ss[:], in0=ss[:], scalar1=1.0 / hd, scalar2=eps, op0=ALU.mult, op1=ALU.add)
    rsqrt_ip(X, ss[:])
    X.dve.tensor_tensor(out=tv, in0=xv, in1=ss[:].unsqueeze(2).to_broadcast([128, H, hd]), op=ALU.mult)
    X.dve.tensor_tensor(out=ov, in0=tv, in1=gt[:, 0:hd].unsqueeze(1).to_broadcast([128, H, hd]), op=ALU.mult)


def rope(X, xt, H, cos, sin, out, tmp):
    xv = xt[:, 0:H * 128].rearrange("p (h d) -> p h d", h=H)
    ov = out[:, 0:H * 128].rearrange("p (h d) -> p h d", h=H)
    tv = tmp[:, 0:H * 128].rearrange("p (h d) -> p h d", h=H)
    cb = cos[:, 0:64].unsqueeze(1).to_broadcast([128, H, 64])
    sb_ = sin[:, 0:64].unsqueeze(1).to_broadcast([128, H, 64])
    x1, x2 = xv[:, :, 0:64], xv[:, :, 64:128]
    X.dve.tensor_tensor(out=ov[:, :, 0:64], in0=x1, in1=cb, op=ALU.mult)
    X.dve.tensor_tensor(out=tv[:, :, 0:64], in0=x2, in1=sb_, op=ALU.mult)
    X.dve.tensor_tensor(out=ov[:, :, 64:128], in0=x2, in1=cb, op=ALU.mult)
    X.dve.tensor_tensor(out=tv[:, :, 64:128], in0=x1, in1=sb_, op=ALU.mult)
    X.dve.tensor_tensor(out=ov[:, :, 0:64], in0=ov[:, :, 0:64], in1=tv[:, :, 0:64], op=ALU.subtract)
    X.dve.tensor_tensor(out=ov[:, :, 64:128], in0=ov[:, :, 64:128], in1=tv[:, :, 64:128], op=ALU.add)


def attn_seq(X, C, K, A, l, seq):
    tok0, T, is_s, si = seq
    NT = T // 128
    NPT = (C.PAST // 128) if is_s else 0
    NS = NT + NPT
    G = C.AH // C.KVH
    HD = C.HD
    with ExitStack() as st:
        kT = X.sb(st, "akT", [128, C.KVH, NS * 128], BF16)
        vA = X.sb(st, "avA", [128, NS, C.KVH, 130], BF16)
        qT = X.sb(st, "aqT", [128, C.KVH, NT, G * 128], BF16)
        with ExitStack() as st1:
            gq = X.sb(st1, "agq", [128, HD])
            gk = X.sb(st1, "agk", [128, HD])
            load_rows(X, gq[:], A["g_qk"][l, 0, :])
            load_rows(X, gk[:], A["g_qk"][l, 1, :])
            X.pool.memset(ap=vA[:], constant=1.0)
            ps_tr = [X.ps(st1, "atr%d" % i, [128, 8, 128], BF16) for i in range(2)]
            tr = [0]
            kc = [X.sb(st1, "akc%d" % i, [128, C.KV_W], BF16) for i in range(2)]
            for i in range(NPT):
                b = i % 2
                X.pool.dma_start(out=kc[b][:], in_=A["cache_k"][l, i * 128:(i + 1) * 128, :])
                X.pool.dma_start(out=vA[:, i, :, 0:128],
                                 in_=A["cache_v"][l, i * 128:(i + 1) * 128, :].rearrange("p (k d) -> p k d", k=C.KVH),
                                 dsem="L_avA")
                pt = ps_tr[tr[0] % 2]
                tr[0] += 1
                for kv in range(C.KVH):
                    X.pe.transpose(out=pt[:, kv, :], in_=kc[b][:, kv * 128:(kv + 1) * 128], identity=K["ident_bf"][:])
                X.dve.tensor_copy(out=kT[:, :, i * 128:(i + 1) * 128], in_=pt[:, 0:C.KVH, :])
            qf = [X.sb(st1, "aqf%d" % i, [128, C.ATT_W]) for i in range(2)]
            kf = [X.sb(st1, "akf%d" % i, [128, C.KV_W]) for i in range(2)]
            t1 = X.sb(st1, "at1", [128, C.ATT_W])
            t2 = X.sb(st1, "at2", [128, C.ATT_W])
            qb = X.sb(st1, "aqb", [128, C.ATT_W], BF16)
            kb = X.sb(st1, "akb", [128, C.KV_W], BF16)
            cs = [X.sb(st1, "acs%d" % i, [128, 64]) for i in range(2)]
            sn = [X.sb(st1, "asn%d" % i, [128, 64]) for i in range(2)]
            for i in range(NT):
                b = i % 2
                r0 = tok0 + i * 128
                X.sp.dma_start(out=qf[b][:], in_=A["PROJ"][r0:r0 + 128, C.O_AQ:C.O_AQ + C.ATT_W])
                X.sp.dma_start(out=kf[b][:], in_=A["PROJ"][r0:r0 + 128, C.O_AK:C.O_AK + C.KV_W])
                X.pool.dma_start(out=vA[:, NPT + i, :, 0:128],
                                 in_=A["PROJ"][r0:r0 + 128, C.O_AV:C.O_AV + C.KV_W].rearrange("p (k d) -> p k d", k=C.KVH),
                                 dsem="L_avA")
                if is_s:
                    X.sp.dma_start(out=cs[b][:], in_=A["c_cos"][i * 128:(i + 1) * 128, :])
                    X.sp.dma_start(out=sn[b][:], in_=A["c_sin"][i * 128:(i + 1) * 128, :])
                head_rmsnorm(X, st1, kf[b], C.KVH, HD, gk, 1e-6, t1, "akn%d" % i, t2)
                if not is_s:
                    X.act.dma_start(out=A["o_k"][si, l, i * 128:(i + 1) * 128, :], in_=t1[:, 0:C.KV_W])
                    X.act.dma_start(out=A["o_v"][si, l, i * 128:(i + 1) * 128, :],
                                    in_=A["PROJ"][r0:r0 + 128, C.O_AV:C.O_AV + C.KV_W])
                    X.act.activation(out=kb[:], in_=t1[:, 0:C.KV_W], func=AF.Copy)
                else:
                    rope(X, t1, C.KVH, cs[b], sn[b], kf[b], t2)
                    X.act.activation(out=kb[:], in_=kf[b][:], func=AF.Copy)
                pt = ps_tr[tr[0] % 2]
                tr[0] += 1
                for kv in range(C.KVH):
                    X.pe.transpose(out=pt[:, kv, :], in_=kb[:, kv * 128:(kv + 1) * 128], identity=K["ident_bf"][:])
                X.dve.tensor_copy(out=kT[:, :, (NPT + i) * 128:(NPT + i + 1) * 128], in_=pt[:, 0:C.KVH, :])
                head_rmsnorm(X, st1, qf[b], C.AH, HD, gq, 1e-6, t1, "aqn%d" % i, t2)
                if is_s:
                    rope(X, t1, C.AH, cs[b], sn[b], qf[b], t2)
                    X.act.activation(out=qb[:], in_=qf[b][:], func=AF.Copy)
                else:
                    X.act.activation(out=qb[:], in_=t1[:], func=AF.Copy)
                for h0 in range(0, C.AH, 8):
                    nh = min(8, C.AH - h0)
                    pt = ps_tr[tr[0] % 2]
                    tr[0] += 1
                    for j in range(nh):
                        X.pe.transpose(out=pt[:, j, :], in_=qb[:, (h0 + j) * 128:(h0 + j + 1) * 128], identity=K["ident_bf"][:])
                    X.dve.tensor_copy(out=qT[:, h0 // G:(h0 + nh) // G, i, :],
                                      in_=pt[:, 0:nh, :].rearrange("p (a g) c -> p a (g c)", g=G))
            X.flush()
        ps_s = [X.ps(st, "aps%d" % i, [128, 512]) for i in range(2)]
        ps_o = [X.ps(st, "apo%d" % i, [128, 512]) for i in range(G)]
        pT = [X.sb(st, "apT%d" % i, [128, G * 128], BF16) for i in range(2)]
        ot = [X.sb(st, "aot%d" % i, [128, G, 128]) for i in range(2)]
        rec = X.sb(st, "arec", [128, G])
        c = 0
        oc = 0
        for kv in range(C.KVH):
            for qbk in range(NT):
                for s in range(NS):
                    ps = ps_s[c % 2]
                    p = pT[c % 2]
                    c += 1
                    X.pe.matmul(out=ps[:, 0:G * 128], lhsT=kT[:, kv, s * 128:(s + 1) * 128], rhs=qT[:, kv, qbk, :],
                                start=True, stop=True)
                    X.act.activation(out=p[:], in_=ps[:, 0:G * 128], func=AF.Exp, scale=float(HD) ** -0.5)
                    for g in range(G):
                        X.pe.matmul(out=ps_o[g][:, 0:129], lhsT=p[:, g * 128:(g + 1) * 128], rhs=vA[:, s, kv, 0:129],
                                    start=(s == 0), stop=(s == NS - 1))
                o = ot[oc % 2]
                oc += 1
                for g in range(G):
                    X.dve.reciprocal(out=rec[:, g:g + 1], in_=ps_o[g][:, 128:129])
                    X.dve.tensor_scalar(out=o[:, g, :], in0=ps_o[g][:, 0:128], scalar1=rec[:, g:g + 1], scalar2=None,
                                        op0=ALU.mult)
                r0 = tok0 + qbk * 128
                X.sp.dma_start(out=A["MIX"][r0:r0 + 128, kv * G * 128:(kv + 1) * G * 128],
                               in_=o[:].rearrange("p g d -> p (g d)"))
        X.flush()


def full_mixers(X, C, K, A, l):
    for seq in C.seqs:
        attn_seq(X, C, K, A, l, seq)
        mlstm_seq(X, C, K, A, l, seq)
        rwkv_seq(X, C, K, A, l, seq)


N_CORES = 8


def kernel(**inputs):
    C = Cfg()
    L = C.DEPTH
    inp = {k: np.asarray(v) for k, v in inputs.items()}
    nc = build(C, mixers_fn=full_mixers)
    consts = make_consts(C)
    shp = input_shapes(C)
    shared = {k: np.ascontiguousarray(inp[k], dtype=np.float32).reshape(shp[k]) for k in W_NAMES}
    shared.update(consts)
    cpb = N_CORES // inp["x_sample"].shape[0]
    in_maps = []
    for core in range(N_CORES):
        b = core // cpb
        xs = inp["x_prompt"][core * C.NSEQ:(core + 1) * C.NSEQ].reshape(C.NSEQ * C.TP, C.D)
        d = dict(shared)
        d["x_in"] = np.ascontiguousarray(np.concatenate([xs, inp["x_sample"][b]], axis=0), dtype=np.float32)
        d["cond2"] = np.ascontiguousarray(np.stack([inp["c_ctx"], inp["c"][b]]), dtype=np.float32)
        d["cache_k"] = np.ascontiguousarray(inp["cache_attn_k"][b].reshape(L, C.PAST, C.KV_W), dtype=np.float32)
        d["cache_v"] = np.ascontiguousarray(inp["cache_attn_v"][b].reshape(L, C.PAST, C.KV_W), dtype=np.float32)
        d["st_C"] = np.ascontiguousarray(inp["state_mlstm_C"][b], dtype=np.float32)
        d["st_n"] = np.ascontiguousarray(inp["state_mlstm_n"][b], dtype=np.float32)
        d["st_m"] = np.ascontiguousarray(inp["state_mlstm_m"][b], dtype=np.float32)
        d["st_S"] = np.ascontiguousarray(inp["state_rwkv"][b], dtype=np.float32)
        in_maps.append(d)
    res = run_bass_kernel_spmd(nc, in_maps, core_ids=list(range(N_CORES))).results
    NP = C.NSEQ * C.TP
    y_prompt = np.concatenate([res[c]["y"][:NP].reshape(C.NSEQ, C.TP, C.D) for c in range(N_CORES)], axis=0)
    y_sample = np.stack([res[b * cpb]["y"][NP:] for b in range(inp["x_sample"].shape[0])], axis=0)
    cat = lambda k: np.concatenate([res[c][k] for c in range(N_CORES)], axis=0)
    o_k = cat("o_k").reshape(-1, L, C.TP, C.KVH, C.HD)
    o_v = cat("o_v").reshape(-1, L, C.TP, C.KVH, C.HD)
    return (y_prompt.astype(np.float32), y_sample.astype(np.float32), o_k.astype(np.float32), o_v.astype(np.float32),
            cat("o_C").astype(np.float32), cat("o_n").astype(np.float32), cat("o_m").astype(np.float32),
            cat("o_S").astype(np.float32))


def mlstm_seq(X, C, K, A, l, seq):
    tok0, T, is_s, si = seq
    NT = T // 128
    H = C.MH
    DV = C.MDV
    MK = K["masks"]
    ident = K["ident_f"]
    ones = K["ones_f"]
    with ExitStack() as st:
        qT = X.sb(st, "mqT", [128, H, T], BF16)
        kT = X.sb(st, "mkT", [128, H, T], BF16)
        kf = X.sb(st, "mkf", [128, NT, H * 128])
        vA = X.sb(st, "mvA", [128, NT, H, DV + 2], BF16)
        LI = X.sb(st, "mLI", [128, NT, 2, H])
        LF = X.sb(st, "mLF", [128, NT, 2, H])
        HM = X.sb(st, "mHM", [128, NT, H * DV])
        with ExitStack() as st1:
            ps_tr = [X.ps(st1, "mtr%d" % i, [128, 8, 128], BF16) for i in range(2)]
            bg = X.sb(st1, "mbg", [128, 4 * H])
            load_rows(X, bg[:], A["b_mgate"][l, :])
            X.pool.memset(ap=vA[:], constant=1.0)
            qf = [X.sb(st1, "mqf%d" % i, [128, H * 128]) for i in range(2)]
            mg = [X.sb(st1, "mmg%d" % i, [128, 4 * H]) for i in range(2)]
            gt = X.sb(st1, "mgt", [128, 4 * H])
            ex = X.sb(st1, "mex", [128, 2, H])
            qb = X.sb(st1, "mqb", [128, H * 128], BF16)
            kb = X.sb(st1, "mkb", [128, H * 128], BF16)
            for i in range(NT):
                b = i % 2
                r0 = tok0 + i * 128
                X.sp.dma_start(out=qf[b][:], in_=A["PROJ"][r0:r0 + 128, C.O_MQ:C.O_MQ + H * 128])
                X.sp.dma_start(out=kf[:, i, :], in_=A["PROJ"][r0:r0 + 128, C.O_MK:C.O_MK + H * 128], dsem="L_mkf")
                X.pool.dma_start(out=vA[:, i, :, 0:DV],
                                 in_=A["PROJ"][r0:r0 + 128, C.O_MV:C.O_MV + H * DV].rearrange("p (h d) -> p h d", h=H),
                                 dsem="L_mvA")
                X.sp.dma_start(out=mg[b][:], in_=A["PROJ"][r0:r0 + 128, C.O_MG:C.O_MG + 4 * H])
                X.dve.tensor_tensor(out=gt[:], in0=mg[b][:], in1=bg[:], op=ALU.add)
                gv = gt[:].rearrange("p (d t h) -> p d t h", d=2, t=2)
                X.dve.tensor_copy(out=LI[:, i, :, :], in_=gv[:, :, 0, :])
                X.act.activation(out=ex[:], in_=gv[:, :, 1, :], func=AF.Exp, scale=-1.0)
                X.dve.tensor_scalar(out=ex[:], in0=ex[:], scalar1=1.0, scalar2=None, op0=ALU.add)
                X.act.activation(out=ex[:], in_=ex[:], func=AF.Ln)
                X.dve.tensor_scalar(out=LF[:, i, :, :], in0=ex[:], scalar1=-1.0, scalar2=None, op0=ALU.mult)
                X.act.activation(out=qb[:], in_=qf[b][:], func=AF.Copy, scale=float(C.MDQK) ** -0.5)
                X.act.activation(out=kb[:], in_=kf[:, i, :], func=AF.Copy)
                pt = ps_tr[0]
                for h in range(H):
                    X.pe.transpose(out=pt[:, h, :], in_=qb[:, h * 128:(h + 1) * 128], identity=K["ident_bf"][:])
                X.dve.tensor_copy(out=qT[:, :, i * 128:(i + 1) * 128], in_=pt[:, 0:H, :])
                pt = ps_tr[1]
                for h in range(H):
                    X.pe.transpose(out=pt[:, h, :], in_=kb[:, h * 128:(h + 1) * 128], identity=K["ident_bf"][:])
                X.dve.tensor_copy(out=kT[:, :, i * 128:(i + 1) * 128], in_=pt[:, 0:H, :])
            X.flush()
        with ExitStack() as st1:
            Cn = X.sb(st1, "mCn", [128, H, DV + 1])
            Cnb = X.sb(st1, "mCnb", [128, H, DV + 2], BF16)
            mm = X.sb(st1, "mmm", [128, H])
            ps_cs = X.ps(st1, "mpcs", [128, 2 * H])
            ps_A = X.ps(st1, "mpA", [128, H * 128])
            ps_G = X.ps(st1, "mpG", [128, H * 128])
            ps_qk = X.ps(st1, "mpqk", [128, H * 128])
            ps_in = X.ps(st1, "mpin", [128, 512])
            ps_it = X.ps(st1, "mpit", [128, 512])
            ps_up = X.ps(st1, "mpup", [128, 512])
            cs = X.sb(st1, "mcs", [128, 2 * H])
            ib = X.sb(st1, "mib", [128, H])
            dg = X.sb(st1, "mdg", [128, H, 128])
            tA = X.sb(st1, "mtA", [128, H, 128])
            mx = X.sb(st1, "mmx", [128, H])
            mxa = X.sb(st1, "mmxa", [128, H])
            g = X.sb(st1, "mg_", [128, H])
            ng = X.sb(st1, "mng", [128, H])
            dw = X.sb(st1, "mdw", [128, H, 128])
            wT = X.sb(st1, "mwT", [128, H * 128], BF16)
            gi = X.sb(st1, "mgi", [128, H])
            ee = X.sb(st1, "mee", [128, H])
            nd = X.sb(st1, "mnd", [128, DV + 1])
            rc = X.sb(st1, "mrc", [128, 1])
            gm = X.sb(st1, "mgm", [128, H])
            uu = X.sb(st1, "muu", [128, H])
            dec = X.sb(st1, "mdec", [128, H])
            ku = X.sb(st1, "mku", [128, 128], BF16)
            identb = ident[:].unsqueeze(1).to_broadcast([128, H, 128])
            for d in range(2):
                if is_s:
                    for h in range(H):
                        X.sp.dma_start(out=Cn[:, h, 0:DV], in_=A["st_C"][l, d, h, :, :], dsem="L_mCn")
                        X.sp.dma_start(out=Cn[:, h, DV:DV + 1],
                                       in_=A["st_n"][l, d, h, :].rearrange("(p o) -> p o", o=1), dsem="L_mCn")
                    X.sp.dma_start(out=mm[:], in_=A["st_m"][l, d, :].partition_broadcast(128))
                else:
                    X.dve.memset(ap=Cn[:], constant=0.0)
                    X.dve.memset(ap=mm[:], constant=0.0)
                X.act.activation(out=Cnb[:, :, 0:DV + 1], in_=Cn[:], func=AF.Copy)
                order = range(NT) if d == 0 else range(NT - 1, -1, -1)
                for c in order:
                    cs0 = c * 128
                    lf = LF[:, c, d, :]
                    li = LI[:, c, d, :]
                    X.pe.matmul(out=ps_cs[:, 0:H], lhsT=MK[:, M_INCL + d, :], rhs=lf, start=True, stop=True)
                    X.pe.matmul(out=ps_cs[:, H:2 * H], lhsT=ones[:], rhs=lf, start=True, stop=True)
                    X.act.activation(out=cs[:], in_=ps_cs[:], func=AF.Copy)
                    X.dve.tensor_tensor(out=ib[:], in0=li, in1=cs[:, 0:H], op=ALU.subtract)
                    X.dve.tensor_tensor(out=dg[:], in0=identb, in1=ib[:].unsqueeze(2).to_broadcast([128, H, 128]), op=ALU.mult)
                    X.pe.matmul(out=ps_A[:], lhsT=ones[:], rhs=dg[:].rearrange("p h s -> p (h s)"), start=True, stop=True)
                    pAv = ps_A[:].rearrange("p (h s) -> p h s", h=H)
                    X.dve.tensor_tensor(out=tA[:], in0=pAv, in1=MK[:, M_NEG + d, :].unsqueeze(1).to_broadcast([128, H, 128]), op=ALU.add)
                    X.dve.tensor_reduce(out=mx[:], in_=tA[:], axis=AX.X, op=ALU.max)
                    X.dve.tensor_reduce(out=mxa[:], in_=pAv, axis=AX.X, op=ALU.max)
                    X.dve.tensor_tensor(out=g[:], in0=mx[:], in1=mm[:], op=ALU.max)
                    X.dve.tensor_scalar(out=ng[:], in0=g[:], scalar1=-1.0, scalar2=None, op0=ALU.mult)
                    X.dve.tensor_tensor(out=dg[:], in0=identb, in1=ng[:].unsqueeze(2).to_broadcast([128, H, 128]), op=ALU.mult)
                    X.pe.matmul(out=ps_G[:], lhsT=ones[:], rhs=dg[:].rearrange("p h s -> p (h s)"), start=True, stop=True)
                    X.dve.tensor_tensor(out=tA[:], in0=ps_G[:].rearrange("p (h s) -> p h s", h=H),
                                        in1=MK[:, M_NEGT + d, :].unsqueeze(1).to_broadcast([128, H, 128]), op=ALU.add)
                    for h in range(H):
                        X.act.activation(out=dw[:, h, :], in_=tA[:, h, :], func=AF.Exp, bias=ib[:, h:h + 1], scale=1.0)
                        X.pe.matmul(out=ps_qk[:, h * 128:(h + 1) * 128], lhsT=kT[:, h, cs0:cs0 + 128], rhs=qT[:, h, cs0:cs0 + 128],
                                    start=True, stop=True)
                    X.dve.tensor_tensor(out=wT[:], in0=ps_qk[:], in1=dw[:].rearrange("p h s -> p (h s)"), op=ALU.mult)
                    X.dve.tensor_tensor(out=gi[:], in0=mm[:], in1=g[:], op=ALU.subtract)
                    X.act.activation(out=gi[:], in_=gi[:], func=AF.Exp)
                    X.dve.tensor_tensor(out=ee[:], in0=cs[:, 0:H], in1=g[:], op=ALU.add)
                    X.act.activation(out=ee[:], in_=ee[:], func=AF.Exp, scale=-1.0)
                    for h in range(H):
                        X.pe.matmul(out=ps_in[:, 0:DV + 1], lhsT=wT[:, h * 128:(h + 1) * 128], rhs=vA[:, c, h, 0:DV + 1],
                                    start=True, stop=True)
                        X.pe.matmul(out=ps_it[:, 0:DV + 1], lhsT=qT[:, h, cs0:cs0 + 128], rhs=Cnb[:, h, 0:DV + 1],
                                    start=True, stop=True)
                        X.act.activation(out=nd[:], in_=ps_in[:, 0:DV + 1], func=AF.Copy)
                        X.dve.scalar_tensor_tensor(out=nd[:], in0=ps_it[:, 0:DV + 1], scalar=gi[:, h:h + 1], in1=nd[:],
                                                   op0=ALU.mult, op1=ALU.add)
                        X.act.activation(out=rc[:], in_=nd[:, DV:DV + 1], func=AF.Abs)
                        X.dve.tensor_tensor(out=rc[:], in0=rc[:], in1=ee[:, h:h + 1], op=ALU.max)
                        X.dve.reciprocal(out=rc[:], in_=rc[:])
                        hm = HM[:, c, h * DV:(h + 1) * DV]
                        if d == 0:
                            X.dve.tensor_scalar(out=hm, in0=nd[:, 0:DV], scalar1=rc[:, 0:1], scalar2=None, op0=ALU.mult)
                        else:
                            X.dve.scalar_tensor_tensor(out=hm, in0=nd[:, 0:DV], scalar=rc[:, 0:1], in1=hm,
                                                       op0=ALU.mult, op1=ALU.add)
                    X.dve.tensor_tensor(out=gm[:], in0=mm[:], in1=mxa[:], op=ALU.max)
                    X.dve.tensor_tensor(out=uu[:], in0=ib[:], in1=gm[:], op=ALU.subtract)
                    X.act.activation(out=uu[:], in_=uu[:], func=AF.Exp)
                    X.dve.tensor_tensor(out=dec[:], in0=mm[:], in1=gm[:], op=ALU.subtract)
                    X.act.activation(out=dec[:], in_=dec[:], func=AF.Exp)
                    for h in range(H):
                        X.dve.tensor_scalar(out=ku[:], in0=kf[:, c, h * 128:(h + 1) * 128], scalar1=uu[:, h:h + 1], scalar2=None,
                                            op0=ALU.mult)
                        X.pe.matmul(out=ps_up[:, 0:DV + 1], lhsT=ku[:], rhs=vA[:, c, h, 0:DV + 1], start=True, stop=True)
                        X.dve.scalar_tensor_tensor(out=Cn[:, h, :], in0=Cn[:, h, :], scalar=dec[:, h:h + 1], in1=ps_up[:, 0:DV + 1],
                                                   op0=ALU.mult, op1=ALU.add)
                    X.act.activation(out=Cnb[:, :, 0:DV + 1], in_=Cn[:], func=AF.Copy)
                    X.dve.tensor_tensor(out=mm[:], in0=cs[:, H:2 * H], in1=gm[:], op=ALU.add)
                if not is_s:
                    for h in range(H):
                        X.act.dma_start(out=A["o_C"][si, l, d, h, :, :], in_=Cn[:, h, 0:DV])
                        X.act.dma_start(out=A["o_n"][si, l, d, h, :].rearrange("(p o) -> p o", o=1), in_=Cn[:, h, DV:DV + 1])
                    X.act.dma_start(out=A["o_m"][si, l, d:d + 1, :], in_=mm[0:1, :])
            X.flush()
        with ExitStack() as st1:
            gmn = X.sb(st1, "mgmn", [128, H * DV])
            load_rows(X, gmn[:], A["g_mnorm"][l, :])
            mo = [X.sb(st1, "mmo%d" % i, [128, H * DV]) for i in range(2)]
            t1 = X.sb(st1, "mt1", [128, H * DV])
            ot = [X.sb(st1, "mot%d" % i, [128, H * DV]) for i in range(2)]
            ss = X.sb(st1, "mss", [128, H])
            for i in range(NT):
                b = i % 2
                r0 = tok0 + i * 128
                X.sp.dma_start(out=mo[b][:], in_=A["PROJ"][r0:r0 + 128, C.O_MO:C.O_MO + H * DV])
                X.act.activation(out=mo[b][:], in_=mo[b][:], func=AF.Sigmoid)
                hv = HM[:, i, :].rearrange("p (h d) -> p h d", h=H)
                tv = t1[:].rearrange("p (h d) -> p h d", h=H)
                X.dve.tensor_tensor(out=tv, in0=hv, in1=hv, op=ALU.mult)
                X.dve.tensor_reduce(out=ss[:], in_=tv, axis=AX.X, op=ALU.add)
                X.dve.tensor_scalar(out=ss[:], in0=ss[:], scalar1=1.0 / DV, scalar2=1e-6, op0=ALU.mult, op1=ALU.add)
                rsqrt_ip(X, ss[:])
                X.dve.tensor_tensor(out=tv, in0=hv, in1=ss[:].unsqueeze(2).to_broadcast([128, H, DV]), op=ALU.mult)
                X.dve.tensor_tensor(out=t1[:], in0=t1[:], in1=gmn[:], op=ALU.mult)
                X.dve.tensor_tensor(out=ot[b][:], in0=t1[:], in1=mo[b][:], op=ALU.mult)
                X.sp.dma_start(out=A["MIX"][r0:r0 + 128, C.ATT_W:C.ATT_W + H * DV], in_=ot[b][:])
            X.flush()


def rwkv_prep(X, C, K, A, l, seq):
    tok0, T, is_s, si = seq
    NT = T // 128
    RW, RH, RC = C.R_W, C.RH, C.R_COLS
    O = C.O_RP
    with ExitStack() as st:
        mu = X.sb(st, "rmu", [128, RC])
        load_rows(X, mu[:], A["mu_rwkv"][l, :])
        w2sb = X.sb(st, "rw2", [128, RW], BF16)
        a2sb = X.sb(st, "ra2", [128, RW], BF16)
        g2sb = X.sb(st, "rg2", [128, RW], BF16)
        X.pool.dma_start(out=w2sb[:], in_=A["w2_rwkv"][l].rearrange("d k n -> (d k) n"))
        X.pool.dma_start(out=a2sb[:], in_=A["a2_rwkv"][l].rearrange("d k n -> (d k) n"))
        X.pool.dma_start(out=g2sb[:], in_=A["g2_rwkv"][l])
        w0b = X.sb(st, "rw0", [128, 2, RW])
        a0b = X.sb(st, "ra0", [128, 2, RW])
        for d in range(2):
            X.sp.dma_start(out=w0b[:, d, :], in_=A["w0_rwkv"][l, d, :].partition_broadcast(128), dsem="L_rw0")
            X.sp.dma_start(out=a0b[:, d, :], in_=A["a0_rwkv"][l, d, :].partition_broadcast(128), dsem="L_ra0")
        kkb = X.sb(st, "rkkb", [128, RW])
        kab = X.sb(st, "rkab", [128, RW])
        rkb = X.sb(st, "rrkb", [128, RW])
        load_rows(X, kkb[:], A["k_k_rwkv"][l, :])
        load_rows(X, kab[:], A["k_a_rwkv"][l, :])
        load_rows(X, rkb[:], A["r_k_rwkv"][l, :])
        cur = [X.sb(st, "rcur%d" % i, [128, RC]) for i in range(2)]
        prv = X.sb(st, "rprv", [128, RC])
        nxt = X.sb(st, "rnxt", [128, RC])
        la = X.sb(st, "rla", [128, 384], BF16)
        lT = X.sb(st, "rlT", [128, 3, 128], BF16)
        ps_tr = X.ps(st, "rptr", [128, 8, 128], BF16)
        ps_w = X.ps(st, "rpw", [128, RW])
        ps_a = X.ps(st, "rpa", [128, RW])
        t1 = X.sb(st, "rt1", [128, RW])
        t2 = X.sb(st, "rt2", [128, RW])
        asg = X.sb(st, "rasg", [128, RW])
        kkn = X.sb(st, "rkkn", [128, RW])
        bon = X.sb(st, "rbon", [128, RW])
        ssh = X.sb(st, "rssh", [128, RH])
        for i in range(NT):
            b = i % 2
            r0 = tok0 + i * 128
            xs = cur[b]
            X.sp.dma_start(out=xs[:], in_=A["PROJ"][r0:r0 + 128, O:O + RC])
            if i == 0:
                X.dve.memset(ap=prv[:], constant=0.0)
                X.sp.dma_start(out=prv[1:128, :], in_=A["PROJ"][r0:r0 + 127, O:O + RC])
            else:
                X.sp.dma_start(out=prv[:], in_=A["PROJ"][r0 - 1:r0 + 127, O:O + RC])
            if i == NT - 1:
                X.dve.memset(ap=nxt[:], constant=0.0)
                X.sp.dma_start(out=nxt[0:127, :], in_=A["PROJ"][r0 + 1:r0 + 128, O:O + RC])
            else:
                X.sp.dma_start(out=nxt[:], in_=A["PROJ"][r0 + 1:r0 + 129, O:O + RC])
            X.dve.tensor_tensor(out=prv[:], in0=prv[:], in1=nxt[:], op=ALU.add)
            X.dve.scalar_tensor_tensor(out=prv[:], in0=prv[:], scalar=0.5, in1=xs[:], op0=ALU.mult, op1=ALU.subtract)
            X.pool.tensor_tensor(out=prv[:], in0=prv[:], in1=mu[:], op=ALU.mult)
            X.dve.tensor_tensor(out=xs[:], in0=xs[:], in1=prv[:], op=ALU.add)
            rr, kr, vr = xs[:, 0:RW], xs[:, RW:2 * RW], xs[:, 2 * RW:3 * RW]
            lo = 3 * RW
            X.act.dma_start(out=A["RR"][r0:r0 + 128, :], in_=rr)
            X.act.dma_start(out=A["RV"][r0:r0 + 128, :], in_=vr)
            X.act.activation(out=la[:, 0:128], in_=xs[:, lo:lo + 128], func=AF.Tanh)
            X.act.activation(out=la[:, 128:256], in_=xs[:, lo + 128:lo + 256], func=AF.Copy)
            X.act.activation(out=la[:, 256:384], in_=xs[:, lo + 256:lo + 384], func=AF.Sigmoid)
            for j in range(3):
                X.pe.transpose(out=ps_tr[:, j, :], in_=la[:, j * 128:(j + 1) * 128], identity=K["ident_bf"][:])
            X.dve.tensor_copy(out=lT[:], in_=ps_tr[:, 0:3, :])
            X.dve.tensor_tensor(out=t1[:], in0=kr, in1=kkb[:], op=ALU.mult)
            t1v = t1[:].rearrange("p (h k) -> p h k", h=RH)
            t2v = t2[:].rearrange("p (h k) -> p h k", h=RH)
            X.dve.tensor_tensor(out=t2[:], in0=t1[:], in1=t1[:], op=ALU.mult)
            X.dve.tensor_reduce(out=ssh[:], in_=t2v, axis=AX.X, op=ALU.add)
            X.act.activation(out=ssh[:], in_=ssh[:], func=AF.Sqrt)
            X.dve.tensor_scalar(out=ssh[:], in0=ssh[:], scalar1=1e-12, scalar2=None, op0=ALU.max)
            X.dve.reciprocal(out=ssh[:], in_=ssh[:])
            X.dve.tensor_tensor(out=kkn[:].rearrange("p (h k) -> p h k", h=RH), in0=t1v,
                                in1=ssh[:].unsqueeze(2).to_broadcast([128, RH, 64]), op=ALU.mult)
            X.act.dma_start(out=A["RKK"][r0:r0 + 128, :], in_=kkn[:])
            for n0 in range(0, RW, 512):
                X.pe.matmul(out=ps_w[:, n0:min(RW, n0 + 512)], lhsT=lT[:, 2, :], rhs=g2sb[:, n0:min(RW, n0 + 512)], start=True, stop=True)
            X.act.activation(out=t2[:], in_=ps_w[:], func=AF.Copy)
            X.act.dma_start(out=A["RGATE"][r0:r0 + 128, :], in_=t2[:])
            for d in range(2):
                p0 = d * 64
                for n0 in range(0, RW, 512):
                    X.pe.matmul(out=ps_w[:, n0:min(RW, n0 + 512)], lhsT=lT[p0:p0 + 64, 0, :], rhs=w2sb[p0:p0 + 64, n0:min(RW, n0 + 512)],
                                start=True, stop=True)
                    X.pe.matmul(out=ps_a[:, n0:min(RW, n0 + 512)], lhsT=lT[p0:p0 + 64, 1, :], rhs=a2sb[p0:p0 + 64, n0:min(RW, n0 + 512)],
                                start=True, stop=True)
                X.dve.tensor_tensor(out=t1[:], in0=ps_w[:], in1=w0b[:, d, :], op=ALU.add)
                X.act.activation(out=t1[:], in_=t1[:], func=AF.Sigmoid)
                X.dve.tensor_scalar(out=t1[:], in0=t1[:], scalar1=-0.6065306597126334, scalar2=None, op0=ALU.mult)
                X.act.dma_start(out=A["RLW%d" % d][r0:r0 + 128, :], in_=t1[:])
                X.dve.tensor_tensor(out=asg[:], in0=ps_a[:], in1=a0b[:, d, :], op=ALU.add)
                X.act.activation(out=asg[:], in_=asg[:], func=AF.Sigmoid)
                X.dve.tensor_tensor(out=t2[:], in0=kkn[:], in1=asg[:], op=ALU.mult)
                X.act.dma_start(out=A["RB%d" % d][r0:r0 + 128, :], in_=t2[:])
                X.dve.scalar_tensor_tensor(out=t1[:], in0=asg[:], scalar=-1.0, in1=kab[:], op0=ALU.add, op1=ALU.mult)
                X.dve.scalar_tensor_tensor(out=t1[:], in0=t1[:], scalar=1.0, in1=kr, op0=ALU.add, op1=ALU.mult)
                X.act.dma_start(out=A["RKD%d" % d][r0:r0 + 128, :], in_=t1[:])
                X.dve.tensor_tensor(out=t2[:], in0=t1[:], in1=rr, op=ALU.mult)
                X.dve.tensor_tensor(out=t2[:], in0=t2[:], in1=rkb[:], op=ALU.mult)
                X.dve.tensor_reduce(out=ssh[:], in_=t2v, axis=AX.X, op=ALU.add)
                vv = vr.rearrange("p (h k) -> p h k", h=RH)
                bv = bon[:].rearrange("p (h k) -> p h k", h=RH)
                sb_ = ssh[:].unsqueeze(2).to_broadcast([128, RH, 64])
                if d == 0:
                    X.dve.tensor_tensor(out=bv, in0=vv, in1=sb_, op=ALU.mult)
                else:
                    X.dve.tensor_tensor(out=t2v, in0=vv, in1=sb_, op=ALU.mult)
                    X.dve.tensor_tensor(out=bon[:], in0=bon[:], in1=t2[:], op=ALU.add)
            X.act.dma_start(out=A["RBON"][r0:r0 + 128, :], in_=bon[:])
        X.flush()


def rwkv_scan(X, C, K, A, l, seq, d):
    tok0, T, is_s, si = seq
    NT = T // 128
    RW, RH = C.R_W, C.RH
    NP = RH // 2
    MK = K["masks"]
    ident = K["ident_f"]
    with ExitStack() as st:
        H = X.sb(st, "sH", [128, NP, 64])
        Hb = X.sb(st, "sHb", [128, NP, 64], BF16)
        ps_L = X.ps(st, "spL", [128, RW])
        ps_tr = [X.ps(st, "sptr%d" % i, [128, 8, 128], BF16) for i in range(2)]
        ps_sc = X.ps(st, "spsc", [128, 512])
        ps_ivt = X.ps(st, "spiv", [128, 4, 128])
        ps_iv = [ps_ivt[:, i, :] for i in range(4)]
        ps_smt = X.ps(st, "spsm", [128, 512])
        ps_sm = [ps_smt[:, i * 64:(i + 1) * 64] for i in range(3)]
        ps_pc = ps_smt[:, 256:256 + 2 * NP]
        ps_hu = X.ps(st, "sphu", [128, 4, 2, 64])
        if is_s and not DBG.get("no_init"):
            Sv = X.sb(st, "sSv", [64, RH, 64])
            X.sp.dma_start(out=Sv[:], in_=A["st_S"][l, d].rearrange("h v k -> v h k"))
            for hp in range(NP):
                pst = ps_iv[hp % 4]
                X.pe.transpose(out=pst[:, 0:64], in_=Sv[:, 2 * hp:2 * hp + 2, :].rearrange("p a k -> p (a k)"),
                               identity=ident[0:64, 0:64])
                X.dve.tensor_copy(out=H[:, hp, :], in_=pst[:, 0:64])
        else:
            X.dve.memset(ap=H[:], constant=0.0)
        X.act.activation(out=Hb[:], in_=H[:], func=AF.Copy)
        ld = {k: [X.sb(st, "s%s%d" % (k, i), [128, RW]) for i in range(2)] for k in ("r", "v", "kk", "lw", "kd", "bv")}
        src = {"r": "RR", "v": "RV", "kk": "RKK", "lw": "RLW%d" % d, "kd": "RKD%d" % d, "bv": "RB%d" % d}
        Pin = X.sb(st, "sPin", [128, RW])
        Pinv = X.sb(st, "sPinv", [128, RW])
        Pex = X.sb(st, "sPex", [128, RW])
        tok = {k: X.sb(st, "sT%s" % k, [128, RW], BF16) for k in ("a", "r", "b", "k", "v")}
        arT = X.sb(st, "sarT", [128, NP, 2, 128], BF16)
        bkT = X.sb(st, "sbkT", [128, NP, 2, 128], BF16)
        pct = X.sb(st, "spct", [128, NP])
        Xf = X.sb(st, "sXf", [128, 128])
        Xl = X.sb(st, "sXl", [128, 128])
        sc3 = [X.sb(st, "ssc3%d" % i, [128, 3, 128], BF16) for i in range(2)]
        Dm = X.sb(st, "sD", [128, 128])
        DT = X.sb(st, "sDT", [128, 128])
        T1 = X.sb(st, "sT1", [128, 128])
        NTb = X.sb(st, "sNTb", [128, 128], BF16)
        rhsb = X.sb(st, "srhsb", [128, 64], BF16)
        Ub = X.sb(st, "sUb", [128, RH, 64], BF16)
        Yt = [X.sb(st, "sYt%d" % i, [128, RW]) for i in range(2)]
        order = range(NT) if d == 0 else range(NT - 1, -1, -1)
        for ci, c in enumerate(order):
            b = ci % 2
            r0 = tok0 + c * 128
            if DBG.get("pre0"):
                continue
            for k in ld:
                X.sp.dma_start(out=ld[k][b][:], in_=A[src[k]][r0:r0 + 128, :])
            if DBG.get("pre05"):
                continue
            lw = ld["lw"][b]
            for n0 in range(0, RW, 512):
                X.pe.matmul(out=ps_L[:, n0:min(RW, n0 + 512)], lhsT=MK[:, M_INCL + d, :], rhs=lw[:, n0:min(RW, n0 + 512)], start=True, stop=True)
            if DBG.get("p1a"):
                continue
            X.act.activation(out=Pin[:], in_=ps_L[:], func=AF.Exp)
            if DBG.get("p1b"):
                continue
            X.act.activation(out=Pinv[:], in_=ps_L[:], func=AF.Exp, scale=-1.0)
            if DBG.get("p1c"):
                continue
            X.act.activation(out=Pex[:], in_=lw[:], func=AF.Exp, scale=-1.0)
            X.dve.tensor_tensor(out=Pex[:], in0=Pex[:], in1=Pin[:], op=ALU.mult)
            if DBG.get("pre1"):
                continue
            X.dve.tensor_tensor(out=tok["r"][:], in0=ld["r"][b][:], in1=Pin[:], op=ALU.mult)
            X.dve.scalar_tensor_tensor(out=tok["a"][:], in0=ld["kk"][b][:], scalar=-1.0, in1=Pex[:], op0=ALU.mult, op1=ALU.mult)
            X.dve.tensor_tensor(out=tok["b"][:], in0=ld["bv"][b][:], in1=Pinv[:], op=ALU.mult)
            X.dve.tensor_tensor(out=tok["k"][:], in0=ld["kd"][b][:], in1=Pinv[:], op=ALU.mult)
            X.act.activation(out=tok["v"][:], in_=ld["v"][b][:], func=AF.Copy)
            if DBG.get("pre2"):
                continue
            for ti_, (nm, dstT, slot) in enumerate((("a", arT, 0), ("r", arT, 1), ("b", bkT, 0), ("k", bkT, 1))):
                pt = ps_tr[ti_ % 2]
                for hp in range(NP):
                    X.pe.transpose(out=pt[:, hp, :], in_=tok[nm][:, hp * 128:(hp + 1) * 128], identity=K["ident_bf"][:])
                if ti_ % 2 == 0:
                    X.dve.tensor_copy(out=dstT[:, :, slot, :], in_=pt[:, 0:NP, :])
                else:
                    X.act.activation(out=dstT[:, :, slot, :], in_=pt[:, 0:NP, :], func=AF.Copy)
            if DBG.get("pre3"):
                continue
            e0 = 126 if d == 0 else 0
            ecol = 1 if d == 0 else 0
            for hp in range(NP):
                X.pe.matmul(out=ps_pc[:, 2 * hp:2 * hp + 2], lhsT=Pin[:, hp * 128:(hp + 1) * 128], rhs=ident[:, e0:e0 + 2],
                            start=True, stop=True)
            X.act.activation(out=pct[:], in_=ps_pc.rearrange("p (a b) -> p a b", b=2)[:, :, ecol], func=AF.Copy)
            if DBG.get("pre4"):
                continue
            for h in range(RH if not DBG.get("no_heads") else 0):
                hp, h2 = divmod(h, 2)
                p0 = h2 * 64
                s3 = sc3[h % 2]
                aT_h = arT[p0:p0 + 64, hp, 0, :]
                rT_h = arT[p0:p0 + 64, hp, 1, :]
                bT_h = bkT[p0:p0 + 64, hp, 0, :]
                kT_h = bkT[p0:p0 + 64, hp, 1, :]
                ar_h = arT[p0:p0 + 64, hp, :, :].rearrange("p a t -> p (a t)")
                X.pe.matmul(out=ps_sc[:, 0:256], lhsT=bT_h, rhs=ar_h, start=True, stop=True)
                X.pe.matmul(out=ps_sc[:, 256:512], lhsT=kT_h, rhs=ar_h, start=True, stop=True)
                X.pe.matmul(out=ps_iv[0][:], lhsT=aT_h, rhs=bT_h, start=True, stop=True)
                X.dve.tensor_tensor(out=Xf[:], in0=ps_sc[:, 0:128], in1=MK[:, M_STRICT + d, :], op=ALU.mult)
                X.dve.tensor_tensor(out=s3[:], in0=ps_sc[:, 128:512].rearrange("p (a t) -> p a t", a=3),
                                    in1=MK[:, M_ISI + 3 * d:M_ISI + 3 * d + 3, :], op=ALU.mult)
                X.dve.tensor_tensor(out=Dm[:], in0=ps_iv[0][:], in1=MK[:, M_L0T + d, :], op=ALU.mult)
                X.dve.tensor_tensor(out=Dm[:], in0=Dm[:], in1=ident[:], op=ALU.add)
                X.dve.tensor_tensor(out=DT[:], in0=Xf[:], in1=MK[:, M_LVL + 7 * d, :], op=ALU.mult)
                X.dve.tensor_tensor(out=DT[:], in0=DT[:], in1=ident[:], op=ALU.add)
                for lv in range(1, 7 if not DBG.get("no_inv") else 1):
                    X.dve.tensor_tensor(out=Xl[:], in0=Xf[:], in1=MK[:, M_LVL + 7 * d + lv, :], op=ALU.mult)
                    X.pe.matmul(out=ps_iv[1][:], lhsT=Xl[:], rhs=Dm[:], start=True, stop=True)
                    X.dve.tensor_copy(out=T1[:], in_=ps_iv[1][:])
                    if lv < 6:
                        X.pe.matmul(out=ps_iv[2][:], lhsT=DT[:], rhs=T1[:], start=True, stop=True)
                    X.pe.matmul(out=ps_iv[3][:], lhsT=T1[:], rhs=DT[:], start=True, stop=True)
                    if lv < 6:
                        X.dve.tensor_tensor(out=Dm[:], in0=Dm[:], in1=ps_iv[2][:], op=ALU.add)
                    X.dve.tensor_tensor(out=DT[:], in0=DT[:], in1=ps_iv[3][:], op=ALU.add)
                X.act.activation(out=NTb[:], in_=DT[:], func=AF.Copy)
                vb_h = tok["v"][:, h * 64:(h + 1) * 64]
                X.pe.matmul(out=ps_sm[0][:], lhsT=aT_h, rhs=Hb[p0:p0 + 64, hp, :], start=True, stop=False)
                X.pe.matmul(out=ps_sm[0][:], lhsT=s3[:, 1, :], rhs=vb_h, start=False, stop=True)
                X.act.activation(out=rhsb[:], in_=ps_sm[0][:], func=AF.Copy)
                X.pe.matmul(out=ps_sm[1][:], lhsT=NTb[:], rhs=rhsb[:], start=True, stop=True)
                X.act.activation(out=Ub[:, h, :], in_=ps_sm[1][:], func=AF.Copy)
                X.pe.matmul(out=ps_sm[2][:], lhsT=rT_h, rhs=Hb[p0:p0 + 64, hp, :], start=True, stop=False)
                X.pe.matmul(out=ps_sm[2][:], lhsT=s3[:, 0, :], rhs=Ub[:, h, :], start=False, stop=False)
                X.pe.matmul(out=ps_sm[2][:], lhsT=s3[:, 2, :], rhs=vb_h, start=False, stop=True)
                X.act.activation(out=Yt[b][:, h * 64:(h + 1) * 64], in_=ps_sm[2][:], func=AF.Copy)
            X.act.dma_start(out=A["RY%d" % d][r0:r0 + 128, :], in_=Yt[b][:])
            for g4 in range(0, NP if not DBG.get("no_hupd") else 0, 4):
                for j in range(min(4, NP - g4)):
                    hp = g4 + j
                    for h2 in range(2):
                        h = 2 * hp + h2
                        X.pe.matmul(out=ps_hu[:, j, h2, :], lhsT=tok["b"][:, hp * 128:(hp + 1) * 128], rhs=Ub[:, h, :],
                                    start=True, stop=False)
                        X.pe.matmul(out=ps_hu[:, j, h2, :], lhsT=tok["k"][:, hp * 128:(hp + 1) * 128],
                                    rhs=tok["v"][:, h * 64:(h + 1) * 64], start=False, stop=True)
                n4 = min(4, NP - g4)
                X.dve.tensor_tensor(out=H[0:64, g4:g4 + n4, :], in0=H[0:64, g4:g4 + n4, :], in1=ps_hu[0:64, 0:n4, 0, :], op=ALU.add)
                X.dve.tensor_tensor(out=H[64:128, g4:g4 + n4, :], in0=H[64:128, g4:g4 + n4, :], in1=ps_hu[64:128, 0:n4, 1, :],
                                    op=ALU.add)
            X.dve.tensor_tensor(out=H[:], in0=H[:], in1=pct[:].unsqueeze(2).to_broadcast([128, NP, 64]), op=ALU.mult)
            X.act.activation(out=Hb[:], in_=H[:], func=AF.Copy)
        if not is_s and not DBG.get("no_fin"):
            So = X.sb(st, "sSo", [64, NP, 128])
            for hp in range(NP):
                pst = ps_iv[hp % 4]
                X.pe.transpose(out=pst[0:64, :], in_=H[:, hp, :], identity=ident[:])
                X.dve.tensor_copy(out=So[:, hp, :], in_=pst[0:64, :])
            X.act.dma_start(out=A["o_S"][si, l, d].rearrange("(hp h2) v k -> v hp h2 k", h2=2),
                            in_=So[:].rearrange("p a (b k) -> p a b k", b=2))
        X.flush()


def rwkv_final(X, C, K, A, l, seq):
    tok0, T, is_s, si = seq
    NT = T // 128
    RW, RH = C.R_W, C.RH
    with ExitStack() as st:
        lnw = X.sb(st, "flnw", [128, RW])
        lnb = X.sb(st, "flnb", [128, RW])
        load_rows(X, lnw[:], A["ln_x_rwkv"][l, 0, :])
        load_rows(X, lnb[:], A["ln_x_rwkv"][l, 1, :])
        y0 = [X.sb(st, "fy0%d" % i, [128, RW]) for i in range(2)]
        y1 = [X.sb(st, "fy1%d" % i, [128, RW]) for i in range(2)]
        bo = [X.sb(st, "fbo%d" % i, [128, RW]) for i in range(2)]
        ga = [X.sb(st, "fga%d" % i, [128, RW]) for i in range(2)]
        t1 = X.sb(st, "ft1", [128, RW])
        mean = X.sb(st, "fmean", [128, RH])
        var = X.sb(st, "fvar", [128, RH])
        for i in range(NT):
            b = i % 2
            r0 = tok0 + i * 128
            X.sp.dma_start(out=y0[b][:], in_=A["RY0"][r0:r0 + 128, :])
            X.sp.dma_start(out=y1[b][:], in_=A["RY1"][r0:r0 + 128, :])
            X.sp.dma_start(out=bo[b][:], in_=A["RBON"][r0:r0 + 128, :])
            X.sp.dma_start(out=ga[b][:], in_=A["RGATE"][r0:r0 + 128, :])
            y = y0[b]
            yv = y[:].rearrange("p (h k) -> p h k", h=RH)
            tv = t1[:].rearrange("p (h k) -> p h k", h=RH)
            X.dve.tensor_tensor(out=y[:], in0=y[:], in1=y1[b][:], op=ALU.add)
            X.dve.tensor_reduce(out=mean[:], in_=yv, axis=AX.X, op=ALU.add)
            X.dve.tensor_scalar(out=mean[:], in0=mean[:], scalar1=1.0 / 64, scalar2=None, op0=ALU.mult)
            X.dve.tensor_tensor(out=yv, in0=yv, in1=mean[:].unsqueeze(2).to_broadcast([128, RH, 64]), op=ALU.subtract)
            X.dve.tensor_tensor(out=t1[:], in0=y[:], in1=y[:], op=ALU.mult)
            X.dve.tensor_reduce(out=var[:], in_=tv, axis=AX.X, op=ALU.add)
            X.dve.tensor_scalar(out=var[:], in0=var[:], scalar1=1.0 / 64, scalar2=64e-5, op0=ALU.mult, op1=ALU.add)
            rsqrt_ip(X, var[:])
            X.dve.tensor_tensor(out=yv, in0=yv, in1=var[:].unsqueeze(2).to_broadcast([128, RH, 64]), op=ALU.mult)
            X.dve.tensor_tensor(out=y[:], in0=y[:], in1=lnw[:], op=ALU.mult)
            X.dve.tensor_tensor(out=y[:], in0=y[:], in1=lnb[:], op=ALU.add)
            X.dve.tensor_tensor(out=y[:], in0=y[:], in1=bo[b][:], op=ALU.add)
            X.dve.tensor_tensor(out=y[:], in0=y[:], in1=ga[b][:], op=ALU.mult)
            X.act.dma_start(out=A["MIX"][r0:r0 + 128, C.ATT_W + C.MV_W:C.ATT_W + C.MV_W + RW], in_=y[:])
        X.flush()


RWKV_STAGE = [3]
DBG = {}


def rwkv_seq(X, C, K, A, l, seq):
    rwkv_prep(X, C, K, A, l, seq)
    if RWKV_STAGE[0] < 2:
        return
    for d in range(2):
        rwkv_scan(X, C, K, A, l, seq, d)
    if RWKV_STAGE[0] < 3:
        return
    rwkv_final(X, C, K, A, l, seq)
```

```python
import numpy as np
import concourse.bass as bass
import concourse.mybir as mybir
from concourse.bass_utils import run_bass_kernel_spmd
from contextlib import ExitStack

F32 = mybir.dt.float32
BF16 = mybir.dt.bfloat16
ALU = mybir.AluOpType
AF = mybir.ActivationFunctionType
AX = mybir.AxisListType

ENGS = ("pe", "act", "dve", "pool", "sp")


class _Op:
    __slots__ = ("eng", "fn", "deps", "needs_inc", "semval", "is_dma", "dsem", "dval")

    def __init__(self, eng, fn, is_dma, dsem):
        self.eng = eng
        self.fn = fn
        self.deps = set()
        self.needs_inc = False
        self.semval = 0
        self.is_dma = is_dma
        self.dsem = dsem
        self.dval = 0


class _St:
    __slots__ = ("w", "r")

    def __init__(self):
        self.w = None
        self.r = []


class Prog:
    def __init__(self, nc, stack):
        self.nc = nc
        self.ops = []
        self.keys = {}
        self.esem = {e: stack.enter_context(nc.semaphore("es_" + e)) for e in ENGS if e != "sp"}
        self.ecount = {e: 0 for e in self.esem}
        self.dsems = {}
        self.dcount = {}
        self.stack = stack
        self.waited = {e: {} for e in ENGS}
        self.barrier = {}
        self.nphase = 0
        self.ninstr = 0

    def _dsem(self, name, eng="sp"):
        cls = "sw" if eng == "pool" else "hw"
        name = cls + ":" + name
        pm = self.__dict__.setdefault("phase_map", {})
        if name not in pm:
            j = sum(1 for k in pm if k.startswith(cls + ":"))
            phys = "%s%d" % (cls, j)
            if phys not in self.dsems:
                self.dsems[phys] = self.stack.enter_context(self.nc.semaphore("ds_" + phys))
                self.dcount[phys] = 0
            pm[name] = phys
        return pm[name]

    def op(self, eng, fn, reads=(), writes=(), dsem=None):
        is_dma = dsem is not None
        o = _Op(eng, fn, is_dma, dsem)
        if is_dma:
            dsem = self._dsem(dsem, eng)
            o.dsem = dsem
            self.dcount[dsem] += 1
            o.dval = 16 * self.dcount[dsem]
        deps = set()
        for k in reads:
            st = self.keys.get(k)
            if st is not None and st.w is not None:
                deps.add(st.w)
        for k in writes:
            st = self.keys.get(k)
            if st is not None:
                if st.w is not None:
                    deps.add(st.w)
                deps.update(st.r)
        deps.discard(o)
        for k in reads:
            self.keys.setdefault(k, _St()).r.append(o)
        for k in writes:
            st = self.keys.setdefault(k, _St())
            st.w = o
            st.r = []
        o.deps = deps
        for d in deps:
            if not d.is_dma:
                d.needs_inc = True
        self.ops.append(o)
        self.ninstr += 1
        return o

    def flush(self):
        nc = self.nc
        per = {e: [] for e in ENGS}
        for o in self.ops:
            per[o.eng].append(o)
        for e in self.esem:
            for o in reversed(per[e]):
                if not o.is_dma:
                    o.needs_inc = True
                    break
        for e in self.esem:
            c = self.ecount[e]
            for o in per[e]:
                if o.is_dma:
                    continue
                if o.needs_inc:
                    c += 1
                o.semval = c
            self.ecount[e] = c
        barrier = dict(self.barrier)
        esem, dsems, waited = self.esem, self.dsems, self.waited

        def emit(e, eng):
            first = True
            for o in per[e]:
                targets = {}
                if first:
                    for nm, (h, v) in barrier.items():
                        targets[nm] = (h, v)
                    first = False
                for d in o.deps:
                    if d.is_dma:
                        nm, h, v = "d_" + d.dsem, dsems[d.dsem], d.dval
                    else:
                        if d.eng == "pe" and e == "pe":
                            continue
                        nm, h, v = "e_" + d.eng, esem[d.eng], d.semval
                    if nm not in targets or targets[nm][1] < v:
                        targets[nm] = (h, v)
                for nm, (h, v) in targets.items():
                    if v <= 0 or waited[e].get(nm, 0) >= v:
                        continue
                    eng.wait_ge(h, v)
                    waited[e][nm] = v
                ins = o.fn(eng)
                if o.is_dma:
                    ins.then_inc(dsems[o.dsem], 16)
                elif o.needs_inc:
                    ins.then_inc(esem[e], 1)

        with nc.Block() as block:
            if per["pe"]:
                block.tensor(lambda eng: emit("pe", eng))
            if per["act"]:
                block.scalar(lambda eng: emit("act", eng))
            if per["dve"]:
                block.vector(lambda eng: emit("dve", eng))
            if per["pool"]:
                block.gpsimd(lambda eng: emit("pool", eng))
            if per["sp"]:
                block.sync(lambda eng: emit("sp", eng))
        self.barrier = {}
        for e in self.esem:
            if self.ecount[e] > 0:
                self.barrier["e_" + e] = (self.esem[e], self.ecount[e])
        for nm, h in self.dsems.items():
            if self.dcount[nm] > 0:
                self.barrier["d_" + nm] = (h, 16 * self.dcount[nm])
        self.ops = []
        self.keys = {}
        self.phase_map = {}
        self.nphase += 1

    def final_wait(self):
        nc = self.nc
        barrier = dict(self.barrier)
        with nc.Block() as block:
            def f(eng):
                for nm, (h, v) in barrier.items():
                    eng.wait_ge(h, v)
            block.sync(f)


def _is_ap(v):
    return hasattr(v, "tensor") and hasattr(v, "partition_size")


class KAP:
    def __init__(self, ap, key):
        self.ap = ap
        self.key = key

    def __getitem__(self, idx):
        return KAP(self.ap[idx], self.key)


class Eng:
    def __init__(self, P, name):
        self.P = P
        self.name = name

    def __getattr__(self, opname):
        P, ename = self.P, self.name

        def call(**kw):
            reads, writes = [], []
            sb_out = sb_in = None
            for k, v in list(kw.items()):
                okey = None
                if isinstance(v, KAP):
                    okey = v.key
                    v = v.ap
                    kw[k] = v
                if _is_ap(v):
                    if type(v.tensor).__name__ == "DRamTensorHandle":
                        continue
                    key = okey or v.tensor.name
                    if k in ("out", "accum_out", "ap"):
                        writes.append(key)
                        if k == "out":
                            sb_out = key
                    else:
                        reads.append(key)
                        if k == "in_":
                            sb_in = key
            dsem = None
            if opname == "dma_start":
                dsem = kw.pop("dsem", None)
                if dsem is None:
                    nm = sb_out if sb_out is not None else sb_in
                    nm = nm.rsplit("_", 1)[0] if nm is not None else "dd"
                    dsem = ("L_" if sb_out is not None else "S_") + nm
            if opname == "matmul" and not kw.get("start", True):
                reads.append(writes[0])
            if opname == "memset":
                a, c = kw["ap"], kw["constant"]
                fn = lambda e: e.memset(a, c)
            else:
                fn = lambda e: getattr(e, opname)(**kw)
            rec = P.__dict__.get("recording")
            if rec is not None:
                cont = (opname == "matmul" and not kw.get("start", True))
                rec.append((lambda: P.op(ename, fn, reads, writes, dsem=dsem), cont))
                return None
            return P.op(ename, fn, reads, writes, dsem=dsem)

        return call


class Ctx:
    def __init__(self, nc, stack):
        self.nc = nc
        self.P = Prog(nc, stack)
        self.pe = Eng(self.P, "pe")
        self.act = Eng(self.P, "act")
        self.dve = Eng(self.P, "dve")
        self.pool = Eng(self.P, "pool")
        self.sp = Eng(self.P, "sp")
        self.uid = 0

    def sb(self, st, name, shape, dt=F32):
        self.uid += 1
        return st.enter_context(self.nc.sbuf_tensor("%s_%d" % (name, self.uid), list(shape), dt))

    def ps(self, st, name, shape, dt=F32):
        self.uid += 1
        return st.enter_context(self.nc.psum_tensor("%s_%d" % (name, self.uid), list(shape), dt))

    def flush(self):
        self.P.flush()


class Cfg:
    def __init__(self, **kw):
        self.D = 4096
        self.NSEQ = 4
        self.TP = 256
        self.TS = 1024
        self.PAST = 256
        self.DEPTH = 2
        self.AH = 16
        self.KVH = 4
        self.HD = 128
        self.MH = 4
        self.MDQK = 128
        self.MDV = 256
        self.RH = 16
        self.RHD = 64
        self.DFF = 11008
        self.GRID_W = 64
        for k, v in kw.items():
            setattr(self, k, v)
        c = self
        c.ATT_W = c.AH * c.HD
        c.KV_W = c.KVH * c.HD
        c.MQK_W = c.MH * c.MDQK
        c.MV_W = c.MH * c.MDV
        c.MG_W = 4 * c.MH
        c.R_W = c.RH * c.RHD
        c.RL_W = 384
        c.R_COLS = 3 * c.R_W + c.RL_W
        c.MIX_W = c.ATT_W + c.MV_W + c.R_W
        sizes = (c.ATT_W, c.KV_W, c.KV_W, c.MQK_W, c.MQK_W, c.MV_W, c.MV_W, c.MG_W, c.R_COLS)
        offs = np.concatenate([[0], np.cumsum(sizes)])
        (c.O_AQ, c.O_AK, c.O_AV, c.O_MQ, c.O_MK, c.O_MV, c.O_MO, c.O_MG, c.O_RP) = [int(x) for x in offs[:-1]]
        c.IN_COLS = int(offs[-1])
        c.NTOK = c.NSEQ * c.TP + c.TS
        c.groups = [(0, c.NSEQ * c.TP, 0), (c.NSEQ * c.TP, c.TS, 1)]
        c.seqs = [(i * c.TP, c.TP, 0, i) for i in range(c.NSEQ)] + [(c.NSEQ * c.TP, c.TS, 1, 0)]


def cdiv(a, b):
    return (a + b - 1) // b


def gemm(X, hT, KT, toks, jobs, NTILE=512, KSPLIT=1):
    with ExitStack() as st:
        KC = cdiv(KT, KSPLIT)
        NQ = 4 if KC >= 8 else 1
        KQ = cdiv(KC, NQ)
        wb = [[X.sb(st, "wb%d_%d" % (b, q), [128, KQ, NTILE], BF16) for q in range(NQ)] for b in range(2)]
        nps = len(toks) if KSPLIT > 1 else 2
        ps = [X.ps(st, "gps%d" % i, [128, NTILE]) for i in range(nps)]
        cnt = 0
        wcnt = 0
        for (W_ap, n0, nsz, evac) in jobs:
            Wv = W_ap.rearrange("(kt p) n -> p kt n", p=128)
            for kc in range(KSPLIT):
                k0 = kc * KC
                k1 = min(KT, k0 + KC)
                b = wcnt % 2
                wcnt += 1
                for q in range(NQ):
                    a0 = k0 + q * KQ
                    a1 = min(k1, a0 + KQ)
                    if a1 <= a0:
                        continue
                    X.pool.dma_start(out=wb[b][q][:, 0:a1 - a0, 0:nsz], in_=Wv[:, a0:a1, n0:n0 + nsz])
                for ti, (t0, tsz) in enumerate(toks):
                    if KSPLIT > 1:
                        pt = ps[ti]
                    else:
                        pt = ps[cnt % 2]
                        cnt += 1
                    for kt in range(k0, k1):
                        q, r = divmod(kt - k0, KQ)
                        X.pe.matmul(out=pt[0:tsz, 0:nsz], lhsT=hT[:, kt, t0:t0 + tsz], rhs=wb[b][q][:, r, 0:nsz],
                                    start=(kt == 0), stop=(kt == KT - 1))
                    if kc == KSPLIT - 1:
                        evac(ti, n0, nsz, pt)
        X.flush()


def wjobs(W_ap, N, evac, NTILE=512):
    return [(W_ap, j * NTILE, min(NTILE, N - j * NTILE), evac) for j in range(cdiv(N, NTILE))]


def transpose_into(X, ps_tr, src_bf, ncols, hT, col0, ident_bf, alt=[0]):
    KT = cdiv(ncols, 128)
    g = 0
    while g < KT:
        ng = min(8, KT - g)
        pt = ps_tr[alt[0] % 2]
        for i in range(ng):
            kt = g + i
            X.pe.transpose(out=pt[:, i, :], in_=src_bf[:, kt * 128:(kt + 1) * 128], identity=ident_bf[:])
        if alt[0] % 2 == 0:
            X.dve.tensor_copy(out=hT[:, g:g + ng, col0:col0 + 128], in_=pt[:, 0:ng, :])
        else:
            X.act.activation(out=hT[:, g:g + ng, col0:col0 + 128], in_=pt[:, 0:ng, :], func=AF.Copy)
        alt[0] += 1
        g += ng


def rsqrt_ip(X, ap):
    X.act.activation(out=ap, in_=ap, func=AF.Sqrt)
    X.dve.reciprocal(out=ap, in_=ap)


def rms_rstd(X, st, xt, n, eps, junk, name="rs"):
    ss = X.sb(st, name + "ss", [128, 1])
    rs = X.sb(st, name + "r", [128, 1])
    X.act.activation(out=junk, in_=xt, func=AF.Square, accum_out=ss[:])
    X.dve.tensor_scalar(out=rs[:], in0=ss[:], scalar1=1.0 / n, scalar2=eps, op0=ALU.mult, op1=ALU.add)
    rsqrt_ip(X, rs[:])
    return rs


def make_hT(X, C, K, src_ap, ntok, hT, norm=None):
    ncols = src_ap.shape[1]
    with ExitStack() as st:
        ps_tr = [X.ps(st, "ptr%d" % i, [128, 8, 128], BF16) for i in range(2)]
        hb = [X.sb(st, "hb%d" % i, [128, ncols], BF16) for i in range(2)]
        if norm is not None:
            xs = [X.sb(st, "xs%d" % i, [128, ncols]) for i in range(2)]
            tmp = X.sb(st, "ntmp", [128, ncols])
            junk = X.sb(st, "njunk", [128, ncols], BF16)
        for i in range(ntok // 128):
            b = i % 2
            if norm is None:
                X.pool.dma_start(out=hb[b][:], in_=src_ap[i * 128:(i + 1) * 128, :])
            else:
                G, SH = norm
                X.sp.dma_start(out=xs[b][:], in_=src_ap[i * 128:(i + 1) * 128, :])
                rs = rms_rstd(X, st, xs[b][:], ncols, 1e-6, junk[:], name="n%d" % i)
                X.dve.scalar_tensor_tensor(out=tmp[:], in0=xs[b][:], scalar=rs[:, 0:1], in1=G[:], op0=ALU.mult, op1=ALU.mult)
                X.pool.tensor_tensor(out=hb[b][:], in0=tmp[:], in1=SH[:], op=ALU.add)
            transpose_into(X, ps_tr, hb[b], ncols, hT, i * 128, K["ident_bf"])
        X.flush()


def load_rows(X, dst, vec_ap):
    X.sp.dma_start(out=dst, in_=vec_ap.partition_broadcast(128))


def store_evac(X, st, dst_ap, r0, name="ev", engs=("act", "dve")):
    bufs = [X.sb(st, name + "%d" % i, [128, 512]) for i in range(4)]
    cnt = [0]

    def evac(ti, n0, nsz, pt, tsz=128):
        b = bufs[cnt[0] % 4]
        if cnt[0] % 2 == 0:
            X.act.activation(out=b[0:tsz, 0:nsz], in_=pt[0:tsz, 0:nsz], func=AF.Copy)
        else:
            X.dve.tensor_copy(out=b[0:tsz, 0:nsz], in_=pt[0:tsz, 0:nsz])
        cnt[0] += 1
        X.sp.dma_start(out=dst_ap[r0 + ti * 128:r0 + ti * 128 + tsz, n0:n0 + nsz], in_=b[0:tsz, 0:nsz])
    return evac


def stage_mod(X, C, K, A, l):
    KT = C.D // 128
    with ExitStack() as st:
        cf = X.sb(st, "cf", [128, KT, 2])
        scT = X.sb(st, "scT", [128, KT, 2], BF16)
        bm = X.sb(st, "bm", [2, 6 * C.D])
        for c in range(2):
            X.sp.dma_start(out=cf[:, :, c:c + 1], in_=A["cond2"][c:c + 1, :].rearrange("c (kt p) -> p kt c", p=128),
                           allow_slow_non_contiguous=True, dsem="L_cf%d" % c)
        X.act.activation(out=scT[:], in_=cf[:], func=AF.Silu)
        X.sp.dma_start(out=bm[:], in_=A["b_mod"][l, :].partition_broadcast(2))
        ob = [X.sb(st, "mo%d" % i, [2, 512]) for i in range(2)]
        cnt = [0]

        def evac(ti, n0, nsz, pt):
            b = ob[cnt[0] % 2]
            cnt[0] += 1
            X.dve.tensor_tensor(out=b[:, 0:nsz], in0=pt[0:2, 0:nsz], in1=bm[:, n0:n0 + nsz], op=ALU.add)
            X.sp.dma_start(out=A["MOD"][:, n0:n0 + nsz], in_=b[:, 0:nsz])
        gemm(X, scT, KT, [(0, 2)], wjobs(A["w_mod"][l], 6 * C.D, evac))


def mod_tiles(X, C, A, st, l, cond, which, kind):
    D = C.D
    o = 3 * which
    if kind == "norm":
        G = X.sb(st, "G", [128, D])
        SH = X.sb(st, "SH", [128, D])
    else:
        GG = X.sb(st, "GG", [128, D])
    with ExitStack() as tmp:
        t = X.sb(tmp, "mt", [128, D])
        if kind == "norm":
            load_rows(X, t[:], A["MOD"][cond, (o + 1) * D:(o + 2) * D])
            load_rows(X, G[:], A["g_norm"][l, 2 * which, :])
            X.dve.scalar_tensor_tensor(out=G[:], in0=t[:], scalar=1.0, in1=G[:], op0=ALU.add, op1=ALU.mult)
            load_rows(X, SH[:], A["MOD"][cond, o * D:(o + 1) * D])
            res = (G, SH)
        else:
            load_rows(X, GG[:], A["MOD"][cond, (o + 2) * D:(o + 3) * D])
            load_rows(X, t[:], A["g_norm"][l, 2 * which + 1, :])
            X.dve.tensor_tensor(out=GG[:], in0=GG[:], in1=t[:], op=ALU.mult)
            res = GG
        X.flush()
    return res


def resid_pass(X, C, raw_ap, xold_ap, xnew_ap, tok0, ntok, GG):
    D = C.D
    with ExitStack() as st:
        rw = [X.sb(st, "rw%d" % i, [128, D]) for i in range(2)]
        xo = [X.sb(st, "xo%d" % i, [128, D]) for i in range(2)]
        junk = X.sb(st, "rjunk", [128, D], BF16)
        for i in range(ntok // 128):
            b = i % 2
            r0 = tok0 + i * 128
            X.sp.dma_start(out=rw[b][:], in_=raw_ap[r0:r0 + 128, :])
            X.sp.dma_start(out=xo[b][:], in_=xold_ap[r0:r0 + 128, :])
            rs = rms_rstd(X, st, rw[b][:], D, 1e-6, junk[:], name="q%d" % i)
            X.dve.scalar_tensor_tensor(out=rw[b][:], in0=rw[b][:], scalar=rs[:, 0:1], in1=GG[:], op0=ALU.mult, op1=ALU.mult)
            X.pool.tensor_tensor(out=xo[b][:], in0=xo[b][:], in1=rw[b][:], op=ALU.add)
            X.act.dma_start(out=xnew_ap[r0:r0 + 128, :], in_=xo[b][:])
        X.flush()


def layer(X, C, K, A, l, x_in, x_out, mixers_fn):
    D = C.D
    KT = D // 128
    stage_mod(X, C, K, A, l)
    for (tok0, ntok, cond) in C.groups:
        with ExitStack() as st:
            hT = X.sb(st, "hT", [128, KT, ntok], BF16)
            with ExitStack() as st2:
                G, SH = mod_tiles(X, C, A, st2, l, cond, 0, "norm")
                make_hT(X, C, K, x_in[tok0:tok0 + ntok, :], ntok, hT, norm=(G, SH))
            with ExitStack() as st2:
                ev = store_evac(X, st2, A["PROJ"], tok0)
                gemm(X, hT, KT, [(i * 128, 128) for i in range(ntok // 128)], wjobs(A["w_in"][l], C.IN_COLS, ev))
    mixers_fn(X, C, K, A, l)
    KTM = C.MIX_W // 128
    for (tok0, ntok, cond) in C.groups:
        with ExitStack() as st:
            hT = X.sb(st, "mT", [128, KTM, ntok], BF16)
            make_hT(X, C, K, A["MIX"][tok0:tok0 + ntok, :], ntok, hT)
            with ExitStack() as st2:
                ev = store_evac(X, st2, A["RAW"], tok0)
                gemm(X, hT, KTM, [(i * 128, 128) for i in range(ntok // 128)], wjobs(A["w_out"][l], D, ev))
        with ExitStack() as st:
            GG = mod_tiles(X, C, A, st, l, cond, 0, "gate")
            resid_pass(X, C, A["RAW"], x_in, A["X1"], tok0, ntok, GG)
    for (tok0, ntok, cond) in C.groups:
        with ExitStack() as st:
            hT = X.sb(st, "h2T", [128, KT, ntok], BF16)
            with ExitStack() as st2:
                G, SH = mod_tiles(X, C, A, st2, l, cond, 1, "norm")
                make_hT(X, C, K, A["X1"][tok0:tok0 + ntok, :], ntok, hT, norm=(G, SH))
            nt = ntok // 128
            with ExitStack() as st2:
                sg = X.sb(st2, "sg", [128, nt, 512])
                hb = [X.sb(st2, "hid%d" % i, [128, 512]) for i in range(2)]
                cnt = [0]

                def ev_gate(ti, n0, nsz, pt):
                    X.act.activation(out=sg[:, ti, 0:nsz], in_=pt[:, 0:nsz], func=AF.Silu)

                def ev_up(ti, n0, nsz, pt):
                    b = hb[cnt[0] % 2]
                    cnt[0] += 1
                    X.dve.tensor_tensor(out=b[:, 0:nsz], in0=pt[:, 0:nsz], in1=sg[:, ti, 0:nsz], op=ALU.mult)
                    X.sp.dma_start(out=A["HID"][tok0 + ti * 128:tok0 + (ti + 1) * 128, n0:n0 + nsz], in_=b[:, 0:nsz])
                toks = [(i * 128, 128) for i in range(nt)]
                jobs = []
                for j in range(cdiv(C.DFF, 512)):
                    n0 = j * 512
                    nsz = min(512, C.DFF - n0)
                    jobs.append((A["w_gate"][l], n0, nsz, ev_gate))
                    jobs.append((A["w_up"][l], n0, nsz, ev_up))
                gemm(X, hT, KT, toks, jobs)
    KTF = C.DFF // 128
    for (tok0, ntok, cond) in C.groups:
        TG = min(512, ntok)
        for s0 in range(0, ntok, TG):
            with ExitStack() as st:
                hT = X.sb(st, "fT", [128, KTF, TG], BF16)
                make_hT(X, C, K, A["HID"][tok0 + s0:tok0 + s0 + TG, :], TG, hT)
                with ExitStack() as st2:
                    ev = store_evac(X, st2, A["RAW"], tok0 + s0)
                    gemm(X, hT, KTF, [(i * 128, 128) for i in range(TG // 128)], wjobs(A["w_down"][l], D, ev, NTILE=256),
                         NTILE=256, KSPLIT=2)
        with ExitStack() as st:
            GG = mod_tiles(X, C, A, st, l, cond, 1, "gate")
            resid_pass(X, C, A["RAW"], A["X1"], x_out, tok0, ntok, GG)


W_NAMES = ["w_mod", "b_mod", "g_norm", "w_in", "g_qk", "b_mgate", "g_mnorm", "mu_rwkv", "w0_rwkv", "w2_rwkv",
           "a0_rwkv", "a2_rwkv", "g2_rwkv", "k_k_rwkv", "k_a_rwkv", "r_k_rwkv", "ln_x_rwkv", "w_out", "w_gate",
           "w_up", "w_down"]


def input_shapes(C):
    L, D = C.DEPTH, C.D
    return {
        "x_in": [C.NTOK, D], "cond2": [2, D],
        "cache_k": [L, C.PAST, C.KV_W], "cache_v": [L, C.PAST, C.KV_W],
        "st_C": [L, 2, C.MH, C.MDQK, C.MDV], "st_n": [L, 2, C.MH, C.MDQK], "st_m": [L, 2, C.MH],
        "st_S": [L, 2, C.RH, C.RHD, C.RHD],
        "w_mod": [L, D, 6 * D], "b_mod": [L, 6 * D], "g_norm": [L, 4, D], "w_in": [L, D, C.IN_COLS],
        "g_qk": [L, 2, C.HD], "b_mgate": [L, 4 * C.MH], "g_mnorm": [L, C.MV_W], "mu_rwkv": [L, C.R_COLS],
        "w0_rwkv": [L, 2, C.R_W], "w2_rwkv": [L, 2, 64, C.R_W], "a0_rwkv": [L, 2, C.R_W],
        "a2_rwkv": [L, 2, 64, C.R_W], "g2_rwkv": [L, 128, C.R_W], "k_k_rwkv": [L, C.R_W], "k_a_rwkv": [L, C.R_W],
        "r_k_rwkv": [L, C.R_W], "ln_x_rwkv": [L, 2, C.R_W], "w_out": [L, C.MIX_W, D], "w_gate": [L, D, C.DFF],
        "w_up": [L, D, C.DFF], "w_down": [L, C.DFF, D],
        "c_ident": [128, 128], "c_masks": [NMASK, 128, 128], "c_cos": [C.TS, 64], "c_sin": [C.TS, 64],
    }


def output_shapes(C):
    L = C.DEPTH
    return {
        "y": [C.NTOK, C.D],
        "o_k": [C.NSEQ, L, C.TP, C.KV_W], "o_v": [C.NSEQ, L, C.TP, C.KV_W],
        "o_C": [C.NSEQ, L, 2, C.MH, C.MDQK, C.MDV], "o_n": [C.NSEQ, L, 2, C.MH, C.MDQK],
        "o_m": [C.NSEQ, L, 2, C.MH], "o_S": [C.NSEQ, L, 2, C.RH, C.RHD, C.RHD],
    }


def scratch_shapes(C):
    rw = {k: [C.NTOK, C.R_W] for k in ("RR", "RV", "RKK", "RLW0", "RLW1", "RKD0", "RKD1", "RB0", "RB1", "RBON", "RGATE",
                                       "RY0", "RY1")}
    return {**rw, "MOD": [2, 6 * C.D], "PROJ": [C.NTOK, C.IN_COLS], "MIX": [C.NTOK, C.MIX_W], "RAW": [C.NTOK, C.D],
            "X1": [C.NTOK, C.D], "X2": [C.NTOK, C.D], "HID": [C.NTOK, C.DFF]}


M_INCL, M_STRICT, M_NEG, M_NEGT, M_LVL, M_ISI, M_L0T = 0, 2, 4, 6, 8, 22, 28
NMASK = 30


def make_consts(C):
    s = np.arange(128)[:, None]
    t = np.arange(128)[None, :]
    masks = np.zeros((NMASK, 128, 128), np.float32)
    for d in range(2):
        before = (s <= t) if d == 0 else (s >= t)
        strict = (s < t) if d == 0 else (s > t)
        masks[M_INCL + d] = before
        masks[M_STRICT + d] = strict
        masks[M_NEGT + d] = np.where(before, 0.0, -1e30)
        masks[M_NEG + d] = np.where(before.T, 0.0, -1e30)
        for lv in range(7):
            b = 1 << lv
            same = (s // (2 * b)) == (t // (2 * b))
            if d == 0:
                m = same & (s % (2 * b) < b) & (t % (2 * b) >= b)
            else:
                m = same & (s % (2 * b) >= b) & (t % (2 * b) < b)
            masks[M_LVL + d * 7 + lv] = m
            if lv == 0:
                masks[M_L0T + d] = m.T
        masks[M_ISI + 3 * d + 0] = before
        masks[M_ISI + 3 * d + 1] = strict
        masks[M_ISI + 3 * d + 2] = before
    rows = C.TS // C.GRID_W
    row = np.repeat(np.arange(rows), C.GRID_W).astype(np.float32)
    col = np.tile(np.arange(C.GRID_W), rows).astype(np.float32)
    inv = (10000.0 ** (-np.arange(0, 64, 2, dtype=np.float32) / 64)).astype(np.float32)
    ang = np.concatenate([row[:, None] * inv, col[:, None] * inv], axis=-1).astype(np.float32)
    return {"c_ident": np.eye(128, dtype=np.float32), "c_masks": masks,
            "c_cos": np.cos(ang).astype(np.float32), "c_sin": np.sin(ang).astype(np.float32)}


def dummy_mixers(X, C, K, A, l):
    X.sp.dma_start(out=A["MIX"][:, :], in_=A["PROJ"][:, 0:C.MIX_W])
    X.flush()


def build(C, mixers_fn=None, debug_outs=()):
    nc = bass.Bass("TRN2", target_bir_lowering=False)
    A = {}
    for k, shp in input_shapes(C).items():
        A[k] = nc.dram_tensor(k, shp, F32, kind="ExternalInput").ap()
    for k, shp in output_shapes(C).items():
        A[k] = nc.dram_tensor(k, shp, F32, kind="ExternalOutput").ap()
    for k, shp in scratch_shapes(C).items():
        kind = "ExternalOutput" if k in debug_outs else "Internal"
        A[k] = nc.dram_tensor(k, shp, F32, kind=kind).ap()
    with ExitStack() as st:
        X = Ctx(nc, st)
        K = {}
        K["ident_f"] = X.sb(st, "identf", [128, 128])
        K["ident_bf"] = X.sb(st, "identb", [128, 128], BF16)
        X.sp.dma_start(out=K["ident_f"][:], in_=A["c_ident"])
        X.dve.tensor_copy(out=K["ident_bf"][:], in_=K["ident_f"][:])
        K["ones_f"] = X.sb(st, "onesf", [128, 128])
        X.dve.memset(ap=K["ones_f"][:], constant=1.0)
        K["masks"] = X.sb(st, "masks", [128, NMASK, 128])
        X.sp.dma_start(out=K["masks"][:], in_=A["c_masks"].rearrange("m p s -> p m s"))
        X.flush()
        mf = mixers_fn or dummy_mixers
        xs = [A["x_in"], A["X2"], A["y"]]
        for l in range(C.DEPTH):
            x_in = A["x_in"] if l == 0 else A["X2"]
            x_out = A["y"] if l == C.DEPTH - 1 else A["X2"]
            layer(X, C, K, A, l, x_in, x_out, mf)
        X.flush()
        X.P.final_wait()
        print("instructions:", X.P.ninstr, "phases:", X.P.nphase, "dma sems:", len(X.P.dsems))
    return nc


def head_rmsnorm(X, st, xt, H, hd, gt, eps, out, name, tmp):
    ss = X.sb(st, name + "ss", [128, H])
    xv = xt[:, 0:H * hd].rearrange("p (h d) -> p h d", h=H)
    tv = tmp[:, 0:H * hd].rearrange("p (h d) -> p h d", h=H)
    ov = out[:, 0:H * hd].rearrange("p (h d) -> p h d", h=H)
    X.dve.tensor_tensor(out=tv, in0=xv, in1=xv, op=ALU.mult)
    X.dve.tensor_reduce(out=ss[:], in_=tv, axis=AX.X, op=ALU.add)
    X.dve.tensor_scalar(out=ss[:], in0=ss[:], scalar1=1.0 / hd, scalar2=eps, op0=ALU.mult, op1=ALU.add)
    rsqrt_ip(X, ss[:])
    X.dve.tensor_tensor(out=tv, in0=xv, in1=ss[:].unsqueeze(2).to_broadcast([128, H, hd]), op=ALU.mult)
    X.dve.tensor_tensor(out=ov, in0=tv, in1=gt[:, 0:hd].unsqueeze(1).to_broadcast([128, H, hd]), op=ALU.mult)


def rope(X, xt, H, cos, sin, out, tmp):
    xv = xt[:, 0:H * 128].rearrange("p (h d) -> p h d", h=H)
    ov = out[:, 0:H * 128].rearrange("p (h d) -> p h d", h=H)
    tv = tmp[:, 0:H * 128].rearrange("p (h d) -> p h d", h=H)
    cb = cos[:, 0:64].unsqueeze(1).to_broadcast([128, H, 64])
    sb_ = sin[:, 0:64].unsqueeze(1).to_broadcast([128, H, 64])
    x1, x2 = xv[:, :, 0:64], xv[:, :, 64:128]
    X.dve.tensor_tensor(out=ov[:, :, 0:64], in0=x1, in1=cb, op=ALU.mult)
    X.dve.tensor_tensor(out=tv[:, :, 0:64], in0=x2, in1=sb_, op=ALU.mult)
    X.dve.tensor_tensor(out=ov[:, :, 64:128], in0=x2, in1=cb, op=ALU.mult)
    X.dve.tensor_tensor(out=tv[:, :, 64:128], in0=x1, in1=sb_, op=ALU.mult)
    X.dve.tensor_tensor(out=ov[:, :, 0:64], in0=ov[:, :, 0:64], in1=tv[:, :, 0:64], op=ALU.subtract)
    X.dve.tensor_tensor(out=ov[:, :, 64:128], in0=ov[:, :, 64:128], in1=tv[:, :, 64:128], op=ALU.add)


def attn_seq(X, C, K, A, l, seq):
    tok0, T, is_s, si = seq
    NT = T // 128
    NPT = (C.PAST // 128) if is_s else 0
    NS = NT + NPT
    G = C.AH // C.KVH
    HD = C.HD
    with ExitStack() as st:
        kT = X.sb(st, "akT", [128, C.KVH, NS * 128], BF16)
        vA = X.sb(st, "avA", [128, NS, C.KVH, 130], BF16)
        qT = X.sb(st, "aqT", [128, C.KVH, NT, G * 128], BF16)
        with ExitStack() as st1:
            gq = X.sb(st1, "agq", [128, HD])
            gk = X.sb(st1, "agk", [128, HD])
            load_rows(X, gq[:], A["g_qk"][l, 0, :])
            load_rows(X, gk[:], A["g_qk"][l, 1, :])
            X.pool.memset(ap=vA[:], constant=1.0)
            ps_tr = [X.ps(st1, "atr%d" % i, [128, 8, 128], BF16) for i in range(2)]
            tr = [0]
            kc = [X.sb(st1, "akc%d" % i, [128, C.KV_W], BF16) for i in range(2)]
            for i in range(NPT):
                b = i % 2
                X.pool.dma_start(out=kc[b][:], in_=A["cache_k"][l, i * 128:(i + 1) * 128, :])
                X.pool.dma_start(out=vA[:, i, :, 0:128],
                                 in_=A["cache_v"][l, i * 128:(i + 1) * 128, :].rearrange("p (k d) -> p k d", k=C.KVH),
                                 dsem="L_avA")
                pt = ps_tr[tr[0] % 2]
                tr[0] += 1
                for kv in range(C.KVH):
                    X.pe.transpose(out=pt[:, kv, :], in_=kc[b][:, kv * 128:(kv + 1) * 128], identity=K["ident_bf"][:])
                X.dve.tensor_copy(out=kT[:, :, i * 128:(i + 1) * 128], in_=pt[:, 0:C.KVH, :])
            qf = [X.sb(st1, "aqf%d" % i, [128, C.ATT_W]) for i in range(2)]
            kf = [X.sb(st1, "akf%d" % i, [128, C.KV_W]) for i in range(2)]
            t1 = X.sb(st1, "at1", [128, C.ATT_W])
            t2 = X.sb(st1, "at2", [128, C.ATT_W])
            qb = X.sb(st1, "aqb", [128, C.ATT_W], BF16)
            kb = X.sb(st1, "akb", [128, C.KV_W], BF16)
            cs = [X.sb(st1, "acs%d" % i, [128, 64]) for i in range(2)]
            sn = [X.sb(st1, "asn%d" % i, [128, 64]) for i in range(2)]
            for i in range(NT):
                b = i % 2
                r0 = tok0 + i * 128
                X.sp.dma_start(out=qf[b][:], in_=A["PROJ"][r0:r0 + 128, C.O_AQ:C.O_AQ + C.ATT_W])
                X.sp.dma_start(out=kf[b][:], in_=A["PROJ"][r0:r0 + 128, C.O_AK:C.O_AK + C.KV_W])
                X.pool.dma_start(out=vA[:, NPT + i, :, 0:128],
                                 in_=A["PROJ"][r0:r0 + 128, C.O_AV:C.O_AV + C.KV_W].rearrange("p (k d) -> p k d", k=C.KVH),
                                 dsem="L_avA")
                if is_s:
                    X.sp.dma_start(out=cs[b][:], in_=A["c_cos"][i * 128:(i + 1) * 128, :])
                    X.sp.dma_start(out=sn[b][:], in_=A["c_sin"][i * 128:(i + 1) * 128, :])
                head_rmsnorm(X, st1, kf[b], C.KVH, HD, gk, 1e-6, t1, "akn%d" % i, t2)
                if not is_s:
                    X.act.dma_start(out=A["o_k"][si, l, i * 128:(i + 1) * 128, :], in_=t1[:, 0:C.KV_W])
                    X.act.dma_start(out=A["o_v"][si, l, i * 128:(i + 1) * 128, :],
                                    in_=A["PROJ"][r0:r0 + 128, C.O_AV:C.O_AV + C.KV_W])
                    X.act.activation(out=kb[:], in_=t1[:, 0:C.KV_W], func=AF.Copy)
                else:
                    rope(X, t1, C.KVH, cs[b], sn[b], kf[b], t2)
                    X.act.activation(out=kb[:], in_=kf[b][:], func=AF.Copy)
                pt = ps_tr[tr[0] % 2]
                tr[0] += 1
                for kv in range(C.KVH):
                    X.pe.transpose(out=pt[:, kv, :], in_=kb[:, kv * 128:(kv + 1) * 128], identity=K["ident_bf"][:])
                X.dve.tensor_copy(out=kT[:, :, (NPT + i) * 128:(NPT + i + 1) * 128], in_=pt[:, 0:C.KVH, :])
                head_rmsnorm(X, st1, qf[b], C.AH, HD, gq, 1e-6, t1, "aqn%d" % i, t2)
                if is_s:
                    rope(X, t1, C.AH, cs[b], sn[b], qf[b], t2)
                    X.act.activation(out=qb[:], in_=qf[b][:], func=AF.Copy)
                else:
                    X.act.activation(out=qb[:], in_=t1[:], func=AF.Copy)
                for h0 in range(0, C.AH, 8):
                    nh = min(8, C.AH - h0)
                    pt = ps_tr[tr[0] % 2]
                    tr[0] += 1
                    for j in range(nh):
                        X.pe.transpose(out=pt[:, j, :], in_=qb[:, (h0 + j) * 128:(h0 + j + 1) * 128], identity=K["ident_bf"][:])
                    X.dve.tensor_copy(out=qT[:, h0 // G:(h0 + nh) // G, i, :],
                                      in_=pt[:, 0:nh, :].rearrange("p (a g) c -> p a (g c)", g=G))
            X.flush()
        ps_s = [X.ps(st, "aps%d" % i, [128, 512]) for i in range(2)]
        ps_o = [X.ps(st, "apo%d" % i, [128, 512]) for i in range(G)]
        pT = [X.sb(st, "apT%d" % i, [128, G * 128], BF16) for i in range(2)]
        ot = [X.sb(st, "aot%d" % i, [128, G, 128]) for i in range(2)]
        rec = X.sb(st, "arec", [128, G])
        c = 0
        oc = 0
        for kv in range(C.KVH):
            for qbk in range(NT):
                for s in range(NS):
                    ps = ps_s[c % 2]
                    p = pT[c % 2]
                    c += 1
                    X.pe.matmul(out=ps[:, 0:G * 128], lhsT=kT[:, kv, s * 128:(s + 1) * 128], rhs=qT[:, kv, qbk, :],
                                start=True, stop=True)
                    X.act.activation(out=p[:], in_=ps[:, 0:G * 128], func=AF.Exp, scale=float(HD) ** -0.5)
                    for g in range(G):
                        X.pe.matmul(out=ps_o[g][:, 0:129], lhsT=p[:, g * 128:(g + 1) * 128], rhs=vA[:, s, kv, 0:129],
                                    start=(s == 0), stop=(s == NS - 1))
                o = ot[oc % 2]
                oc += 1
                for g in range(G):
                    X.dve.reciprocal(out=rec[:, g:g + 1], in_=ps_o[g][:, 128:129])
                    X.dve.tensor_scalar(out=o[:, g, :], in0=ps_o[g][:, 0:128], scalar1=rec[:, g:g + 1], scalar2=None,
                                        op0=ALU.mult)
                r0 = tok0 + qbk * 128
                X.sp.dma_start(out=A["MIX"][r0:r0 + 128, kv * G * 128:(kv + 1) * G * 128],
                               in_=o[:].rearrange("p g d -> p (g d)"))
        X.flush()


def full_mixers(X, C, K, A, l):
    for seq in C.seqs:
        attn_seq(X, C, K, A, l, seq)
        mlstm_seq(X, C, K, A, l, seq)
        rwkv_seq(X, C, K, A, l, seq)


N_CORES = 8


def kernel(**inputs):
    C = Cfg()
    L = C.DEPTH
    inp = {k: np.asarray(v) for k, v in inputs.items()}
    nc = build(C, mixers_fn=full_mixers)
    consts = make_consts(C)
    shp = input_shapes(C)
    shared = {k: np.ascontiguousarray(inp[k], dtype=np.float32).reshape(shp[k]) for k in W_NAMES}
    shared.update(consts)
    cpb = N_CORES // inp["x_sample"].shape[0]
    in_maps = []
    for core in range(N_CORES):
        b = core // cpb
        xs = inp["x_prompt"][core * C.NSEQ:(core + 1) * C.NSEQ].reshape(C.NSEQ * C.TP, C.D)
        d = dict(shared)
        d["x_in"] = np.ascontiguousarray(np.concatenate([xs, inp["x_sample"][b]], axis=0), dtype=np.float32)
        d["cond2"] = np.ascontiguousarray(np.stack([inp["c_ctx"], inp["c"][b]]), dtype=np.float32)
        d["cache_k"] = np.ascontiguousarray(inp["cache_attn_k"][b].reshape(L, C.PAST, C.KV_W), dtype=np.float32)
        d["cache_v"] = np.ascontiguousarray(inp["cache_attn_v"][b].reshape(L, C.PAST, C.KV_W), dtype=np.float32)
        d["st_C"] = np.ascontiguousarray(inp["state_mlstm_C"][b], dtype=np.float32)
        d["st_n"] = np.ascontiguousarray(inp["state_mlstm_n"][b], dtype=np.float32)
        d["st_m"] = np.ascontiguousarray(inp["state_mlstm_m"][b], dtype=np.float32)
        d["st_S"] = np.ascontiguousarray(inp["state_rwkv"][b], dtype=np.float32)
        in_maps.append(d)
    res = run_bass_kernel_spmd(nc, in_maps, core_ids=list(range(N_CORES))).results
    NP = C.NSEQ * C.TP
    y_prompt = np.concatenate([res[c]["y"][:NP].reshape(C.NSEQ, C.TP, C.D) for c in range(N_CORES)], axis=0)
    y_sample = np.stack([res[b * cpb]["y"][NP:] for b in range(inp["x_sample"].shape[0])], axis=0)
    cat = lambda k: np.concatenate([res[c][k] for c in range(N_CORES)], axis=0)
    o_k = cat("o_k").reshape(-1, L, C.TP, C.KVH, C.HD)
    o_v = cat("o_v").reshape(-1, L, C.TP, C.KVH, C.HD)
    return (y_prompt.astype(np.float32), y_sample.astype(np.float32), o_k.astype(np.float32), o_v.astype(np.float32),
            cat("o_C").astype(np.float32), cat("o_n").astype(np.float32), cat("o_m").astype(np.float32),
            cat("o_S").astype(np.float32))


def mlstm_seq(X, C, K, A, l, seq):
    tok0, T, is_s, si = seq
    NT = T // 128
    H = C.MH
    DV = C.MDV
    MK = K["masks"]
    ident = K["ident_f"]
    ones = K["ones_f"]
    with ExitStack() as st:
        qT = X.sb(st, "mqT", [128, H, T], BF16)
        kT = X.sb(st, "mkT", [128, H, T], BF16)
        kf = X.sb(st, "mkf", [128, NT, H * 128])
        vA = X.sb(st, "mvA", [128, NT, H, DV + 2], BF16)
        LI = X.sb(st, "mLI", [128, NT, 2, H])
        LF = X.sb(st, "mLF", [128, NT, 2, H])
        HM = X.sb(st, "mHM", [128, NT, H * DV])
        with ExitStack() as st1:
            ps_tr = [X.ps(st1, "mtr%d" % i, [128, 8, 128], BF16) for i in range(2)]
            bg = X.sb(st1, "mbg", [128, 4 * H])
            load_rows(X, bg[:], A["b_mgate"][l, :])
            X.pool.memset(ap=vA[:], constant=1.0)
            qf = [X.sb(st1, "mqf%d" % i, [128, H * 128]) for i in range(2)]
            mg = [X.sb(st1, "mmg%d" % i, [128, 4 * H]) for i in range(2)]
            gt = X.sb(st1, "mgt", [128, 4 * H])
            ex = X.sb(st1, "mex", [128, 2, H])
            qb = X.sb(st1, "mqb", [128, H * 128], BF16)
            kb = X.sb(st1, "mkb", [128, H * 128], BF16)
            for i in range(NT):
                b = i % 2
                r0 = tok0 + i * 128
                X.sp.dma_start(out=qf[b][:], in_=A["PROJ"][r0:r0 + 128, C.O_MQ:C.O_MQ + H * 128])
                X.sp.dma_start(out=kf[:, i, :], in_=A["PROJ"][r0:r0 + 128, C.O_MK:C.O_MK + H * 128], dsem="L_mkf")
                X.pool.dma_start(out=vA[:, i, :, 0:DV],
                                 in_=A["PROJ"][r0:r0 + 128, C.O_MV:C.O_MV + H * DV].rearrange("p (h d) -> p h d", h=H),
                                 dsem="L_mvA")
                X.sp.dma_start(out=mg[b][:], in_=A["PROJ"][r0:r0 + 128, C.O_MG:C.O_MG + 4 * H])
                X.dve.tensor_tensor(out=gt[:], in0=mg[b][:], in1=bg[:], op=ALU.add)
                gv = gt[:].rearrange("p (d t h) -> p d t h", d=2, t=2)
                X.dve.tensor_copy(out=LI[:, i, :, :], in_=gv[:, :, 0, :])
                X.act.activation(out=ex[:], in_=gv[:, :, 1, :], func=AF.Exp, scale=-1.0)
                X.dve.tensor_scalar(out=ex[:], in0=ex[:], scalar1=1.0, scalar2=None, op0=ALU.add)
                X.act.activation(out=ex[:], in_=ex[:], func=AF.Ln)
                X.dve.tensor_scalar(out=LF[:, i, :, :], in0=ex[:], scalar1=-1.0, scalar2=None, op0=ALU.mult)
                X.act.activation(out=qb[:], in_=qf[b][:], func=AF.Copy, scale=float(C.MDQK) ** -0.5)
                X.act.activation(out=kb[:], in_=kf[:, i, :], func=AF.Copy)
                pt = ps_tr[0]
                for h in range(H):
                    X.pe.transpose(out=pt[:, h, :], in_=qb[:, h * 128:(h + 1) * 128], identity=K["ident_bf"][:])
                X.dve.tensor_copy(out=qT[:, :, i * 128:(i + 1) * 128], in_=pt[:, 0:H, :])
                pt = ps_tr[1]
                for h in range(H):
                    X.pe.transpose(out=pt[:, h, :], in_=kb[:, h * 128:(h + 1) * 128], identity=K["ident_bf"][:])
                X.dve.tensor_copy(out=kT[:, :, i * 128:(i + 1) * 128], in_=pt[:, 0:H, :])
            X.flush()
        with ExitStack() as st1:
            Cn = X.sb(st1, "mCn", [128, H, DV + 1])
            Cnb = X.sb(st1, "mCnb", [128, H, DV + 2], BF16)
            mm = X.sb(st1, "mmm", [128, H])
            ps_cs = X.ps(st1, "mpcs", [128, 2 * H])
            ps_A = X.ps(st1, "mpA", [128, H * 128])
            ps_G = X.ps(st1, "mpG", [128, H * 128])
            ps_qk = X.ps(st1, "mpqk", [128, H * 128])
            ps_in = X.ps(st1, "mpin", [128, 512])
            ps_it = X.ps(st1, "mpit", [128, 512])
            ps_up = X.ps(st1, "mpup", [128, 512])
            cs = X.sb(st1, "mcs", [128, 2 * H])
            ib = X.sb(st1, "mib", [128, H])
            dg = X.sb(st1, "mdg", [128, H, 128])
            tA = X.sb(st1, "mtA", [128, H, 128])
            mx = X.sb(st1, "mmx", [128, H])
            mxa = X.sb(st1, "mmxa", [128, H])
            g = X.sb(st1, "mg_", [128, H])
            ng = X.sb(st1, "mng", [128, H])
            dw = X.sb(st1, "mdw", [128, H, 128])
            wT = X.sb(st1, "mwT", [128, H * 128], BF16)
            gi = X.sb(st1, "mgi", [128, H])
            ee = X.sb(st1, "mee", [128, H])
            nd = X.sb(st1, "mnd", [128, DV + 1])
            rc = X.sb(st1, "mrc", [128, 1])
            gm = X.sb(st1, "mgm", [128, H])
            uu = X.sb(st1, "muu", [128, H])
            dec = X.sb(st1, "mdec", [128, H])
            ku = X.sb(st1, "mku", [128, 128], BF16)
            identb = ident[:].unsqueeze(1).to_broadcast([128, H, 128])
            for d in range(2):
                if is_s:
                    for h in range(H):
                        X.sp.dma_start(out=Cn[:, h, 0:DV], in_=A["st_C"][l, d, h, :, :], dsem="L_mCn")
                        X.sp.dma_start(out=Cn[:, h, DV:DV + 1],
                                       in_=A["st_n"][l, d, h, :].rearrange("(p o) -> p o", o=1), dsem="L_mCn")
                    X.sp.dma_start(out=mm[:], in_=A["st_m"][l, d, :].partition_broadcast(128))
                else:
                    X.dve.memset(ap=Cn[:], constant=0.0)
                    X.dve.memset(ap=mm[:], constant=0.0)
                X.act.activation(out=Cnb[:, :, 0:DV + 1], in_=Cn[:], func=AF.Copy)
                order = range(NT) if d == 0 else range(NT - 1, -1, -1)
                for c in order:
                    cs0 = c * 128
                    lf = LF[:, c, d, :]
                    li = LI[:, c, d, :]
                    X.pe.matmul(out=ps_cs[:, 0:H], lhsT=MK[:, M_INCL + d, :], rhs=lf, start=True, stop=True)
                    X.pe.matmul(out=ps_cs[:, H:2 * H], lhsT=ones[:], rhs=lf, start=True, stop=True)
                    X.act.activation(out=cs[:], in_=ps_cs[:], func=AF.Copy)
                    X.dve.tensor_tensor(out=ib[:], in0=li, in1=cs[:, 0:H], op=ALU.subtract)
                    X.dve.tensor_tensor(out=dg[:], in0=identb, in1=ib[:].unsqueeze(2).to_broadcast([128, H, 128]), op=ALU.mult)
                    X.pe.matmul(out=ps_A[:], lhsT=ones[:], rhs=dg[:].rearrange("p h s -> p (h s)"), start=True, stop=True)
                    pAv = ps_A[:].rearrange("p (h s) -> p h s", h=H)
                    X.dve.tensor_tensor(out=tA[:], in0=pAv, in1=MK[:, M_NEG + d, :].unsqueeze(1).to_broadcast([128, H, 128]), op=ALU.add)
                    X.dve.tensor_reduce(out=mx[:], in_=tA[:], axis=AX.X, op=ALU.max)
                    X.dve.tensor_reduce(out=mxa[:], in_=pAv, axis=AX.X, op=ALU.max)
                    X.dve.tensor_tensor(out=g[:], in0=mx[:], in1=mm[:], op=ALU.max)
                    X.dve.tensor_scalar(out=ng[:], in0=g[:], scalar1=-1.0, scalar2=None, op0=ALU.mult)
                    X.dve.tensor_tensor(out=dg[:], in0=identb, in1=ng[:].unsqueeze(2).to_broadcast([128, H, 128]), op=ALU.mult)
                    X.pe.matmul(out=ps_G[:], lhsT=ones[:], rhs=dg[:].rearrange("p h s -> p (h s)"), start=True, stop=True)
                    X.dve.tensor_tensor(out=tA[:], in0=ps_G[:].rearrange("p (h s) -> p h s", h=H),
                                        in1=MK[:, M_NEGT + d, :].unsqueeze(1).to_broadcast([128, H, 128]), op=ALU.add)
                    for h in range(H):
                        X.act.activation(out=dw[:, h, :], in_=tA[:, h, :], func=AF.Exp, bias=ib[:, h:h + 1], scale=1.0)
                        X.pe.matmul(out=ps_qk[:, h * 128:(h + 1) * 128], lhsT=kT[:, h, cs0:cs0 + 128], rhs=qT[:, h, cs0:cs0 + 128],
                                    start=True, stop=True)
                    X.dve.tensor_tensor(out=wT[:], in0=ps_qk[:], in1=dw[:].rearrange("p h s -> p (h s)"), op=ALU.mult)
                    X.dve.tensor_tensor(out=gi[:], in0=mm[:], in1=g[:], op=ALU.subtract)
                    X.act.activation(out=gi[:], in_=gi[:], func=AF.Exp)
                    X.dve.tensor_tensor(out=ee[:], in0=cs[:, 0:H], in1=g[:], op=ALU.add)
                    X.act.activation(out=ee[:], in_=ee[:], func=AF.Exp, scale=-1.0)
                    for h in range(H):
                        X.pe.matmul(out=ps_in[:, 0:DV + 1], lhsT=wT[:, h * 128:(h + 1) * 128], rhs=vA[:, c, h, 0:DV + 1],
                                    start=True, stop=True)
                        X.pe.matmul(out=ps_it[:, 0:DV + 1], lhsT=qT[:, h, cs0:cs0 + 128], rhs=Cnb[:, h, 0:DV + 1],
                                    start=True, stop=True)
                        X.act.activation(out=nd[:], in_=ps_in[:, 0:DV + 1], func=AF.Copy)
                        X.dve.scalar_tensor_tensor(out=nd[:], in0=ps_it[:, 0:DV + 1], scalar=gi[:, h:h + 1], in1=nd[:],
                                                   op0=ALU.mult, op1=ALU.add)
                        X.act.activation(out=rc[:], in_=nd[:, DV:DV + 1], func=AF.Abs)
                        X.dve.tensor_tensor(out=rc[:], in0=rc[:], in1=ee[:, h:h + 1], op=ALU.max)
                        X.dve.reciprocal(out=rc[:], in_=rc[:])
                        hm = HM[:, c, h * DV:(h + 1) * DV]
                        if d == 0:
                            X.dve.tensor_scalar(out=hm, in0=nd[:, 0:DV], scalar1=rc[:, 0:1], scalar2=None, op0=ALU.mult)
                        else:
                            X.dve.scalar_tensor_tensor(out=hm, in0=nd[:, 0:DV], scalar=rc[:, 0:1], in1=hm,
                                                       op0=ALU.mult, op1=ALU.add)
                    X.dve.tensor_tensor(out=gm[:], in0=mm[:], in1=mxa[:], op=ALU.max)
                    X.dve.tensor_tensor(out=uu[:], in0=ib[:], in1=gm[:], op=ALU.subtract)
                    X.act.activation(out=uu[:], in_=uu[:], func=AF.Exp)
                    X.dve.tensor_tensor(out=dec[:], in0=mm[:], in1=gm[:], op=ALU.subtract)
                    X.act.activation(out=dec[:], in_=dec[:], func=AF.Exp)
                    for h in range(H):
                        X.dve.tensor_scalar(out=ku[:], in0=kf[:, c, h * 128:(h + 1) * 128], scalar1=uu[:, h:h + 1], scalar2=None,
                                            op0=ALU.mult)
                        X.pe.matmul(out=ps_up[:, 0:DV + 1], lhsT=ku[:], rhs=vA[:, c, h, 0:DV + 1], start=True, stop=True)
                        X.dve.scalar_tensor_tensor(out=Cn[:, h, :], in0=Cn[:, h, :], scalar=dec[:, h:h + 1], in1=ps_up[:, 0:DV + 1],
                                                   op0=ALU.mult, op1=ALU.add)
                    X.act.activation(out=Cnb[:, :, 0:DV + 1], in_=Cn[:], func=AF.Copy)
                    X.dve.tensor_tensor(out=mm[:], in0=cs[:, H:2 * H], in1=gm[:], op=ALU.add)
                if not is_s:
                    for h in range(H):
                        X.act.dma_start(out=A["o_C"][si, l, d, h, :, :], in_=Cn[:, h, 0:DV])
                        X.act.dma_start(out=A["o_n"][si, l, d, h, :].rearrange("(p o) -> p o", o=1), in_=Cn[:, h, DV:DV + 1])
                    X.act.dma_start(out=A["o_m"][si, l, d:d + 1, :], in_=mm[0:1, :])
            X.flush()
        with ExitStack() as st1:
            gmn = X.sb(st1, "mgmn", [128, H * DV])
            load_rows(X, gmn[:], A["g_mnorm"][l, :])
            mo = [X.sb(st1, "mmo%d" % i, [128, H * DV]) for i in range(2)]
            t1 = X.sb(st1, "mt1", [128, H * DV])
            ot = [X.sb(st1, "mot%d" % i, [128, H * DV]) for i in range(2)]
            ss = X.sb(st1, "mss", [128, H])
            for i in range(NT):
                b = i % 2
                r0 = tok0 + i * 128
                X.sp.dma_start(out=mo[b][:], in_=A["PROJ"][r0:r0 + 128, C.O_MO:C.O_MO + H * DV])
                X.act.activation(out=mo[b][:], in_=mo[b][:], func=AF.Sigmoid)
                hv = HM[:, i, :].rearrange("p (h d) -> p h d", h=H)
                tv = t1[:].rearrange("p (h d) -> p h d", h=H)
                X.dve.tensor_tensor(out=tv, in0=hv, in1=hv, op=ALU.mult)
                X.dve.tensor_reduce(out=ss[:], in_=tv, axis=AX.X, op=ALU.add)
                X.dve.tensor_scalar(out=ss[:], in0=ss[:], scalar1=1.0 / DV, scalar2=1e-6, op0=ALU.mult, op1=ALU.add)
                rsqrt_ip(X, ss[:])
                X.dve.tensor_tensor(out=tv, in0=hv, in1=ss[:].unsqueeze(2).to_broadcast([128, H, DV]), op=ALU.mult)
                X.dve.tensor_tensor(out=t1[:], in0=t1[:], in1=gmn[:], op=ALU.mult)
                X.dve.tensor_tensor(out=ot[b][:], in0=t1[:], in1=mo[b][:], op=ALU.mult)
                X.sp.dma_start(out=A["MIX"][r0:r0 + 128, C.ATT_W:C.ATT_W + H * DV], in_=ot[b][:])
            X.flush()


def rwkv_prep(X, C, K, A, l, seq):
    tok0, T, is_s, si = seq
    NT = T // 128
    RW, RH, RC = C.R_W, C.RH, C.R_COLS
    O = C.O_RP
    with ExitStack() as st:
        mu = X.sb(st, "rmu", [128, RC])
        load_rows(X, mu[:], A["mu_rwkv"][l, :])
        w2sb = X.sb(st, "rw2", [128, RW], BF16)
        a2sb = X.sb(st, "ra2", [128, RW], BF16)
        g2sb = X.sb(st, "rg2", [128, RW], BF16)
        X.pool.dma_start(out=w2sb[:], in_=A["w2_rwkv"][l].rearrange("d k n -> (d k) n"))
        X.pool.dma_start(out=a2sb[:], in_=A["a2_rwkv"][l].rearrange("d k n -> (d k) n"))
        X.pool.dma_start(out=g2sb[:], in_=A["g2_rwkv"][l])
        w0b = X.sb(st, "rw0", [128, 2, RW])
        a0b = X.sb(st, "ra0", [128, 2, RW])
        for d in range(2):
            X.sp.dma_start(out=w0b[:, d, :], in_=A["w0_rwkv"][l, d, :].partition_broadcast(128), dsem="L_rw0")
            X.sp.dma_start(out=a0b[:, d, :], in_=A["a0_rwkv"][l, d, :].partition_broadcast(128), dsem="L_ra0")
        kkb = X.sb(st, "rkkb", [128, RW])
        kab = X.sb(st, "rkab", [128, RW])
        rkb = X.sb(st, "rrkb", [128, RW])
        load_rows(X, kkb[:], A["k_k_rwkv"][l, :])
        load_rows(X, kab[:], A["k_a_rwkv"][l, :])
        load_rows(X, rkb[:], A["r_k_rwkv"][l, :])
        cur = [X.sb(st, "rcur%d" % i, [128, RC]) for i in range(2)]
        prv = X.sb(st, "rprv", [128, RC])
        nxt = X.sb(st, "rnxt", [128, RC])
        la = X.sb(st, "rla", [128, 384], BF16)
        lT = X.sb(st, "rlT", [128, 3, 128], BF16)
        ps_tr = X.ps(st, "rptr", [128, 8, 128], BF16)
        ps_w = X.ps(st, "rpw", [128, RW])
        ps_a = X.ps(st, "rpa", [128, RW])
        t1 = X.sb(st, "rt1", [128, RW])
        t2 = X.sb(st, "rt2", [128, RW])
        asg = X.sb(st, "rasg", [128, RW])
        kkn = X.sb(st, "rkkn", [128, RW])
        bon = X.sb(st, "rbon", [128, RW])
        ssh = X.sb(st, "rssh", [128, RH])
        for i in range(NT):
            b = i % 2
            r0 = tok0 + i * 128
            xs = cur[b]
            X.sp.dma_start(out=xs[:], in_=A["PROJ"][r0:r0 + 128, O:O + RC])
            if i == 0:
                X.dve.memset(ap=prv[:], constant=0.0)
                X.sp.dma_start(out=prv[1:128, :], in_=A["PROJ"][r0:r0 + 127, O:O + RC])
            else:
                X.sp.dma_start(out=prv[:], in_=A["PROJ"][r0 - 1:r0 + 127, O:O + RC])
            if i == NT - 1:
                X.dve.memset(ap=nxt[:], constant=0.0)
                X.sp.dma_start(out=nxt[0:127, :], in_=A["PROJ"][r0 + 1:r0 + 128, O:O + RC])
            else:
                X.sp.dma_start(out=nxt[:], in_=A["PROJ"][r0 + 1:r0 + 129, O:O + RC])
            X.dve.tensor_tensor(out=prv[:], in0=prv[:], in1=nxt[:], op=ALU.add)
            X.dve.scalar_tensor_tensor(out=prv[:], in0=prv[:], scalar=0.5, in1=xs[:], op0=ALU.mult, op1=ALU.subtract)
            X.pool.tensor_tensor(out=prv[:], in0=prv[:], in1=mu[:], op=ALU.mult)
            X.dve.tensor_tensor(out=xs[:], in0=xs[:], in1=prv[:], op=ALU.add)
            rr, kr, vr = xs[:, 0:RW], xs[:, RW:2 * RW], xs[:, 2 * RW:3 * RW]
            lo = 3 * RW
            X.act.dma_start(out=A["RR"][r0:r0 + 128, :], in_=rr)
            X.act.dma_start(out=A["RV"][r0:r0 + 128, :], in_=vr)
            X.act.activation(out=la[:, 0:128], in_=xs[:, lo:lo + 128], func=AF.Tanh)
            X.act.activation(out=la[:, 128:256], in_=xs[:, lo + 128:lo + 256], func=AF.Copy)
            X.act.activation(out=la[:, 256:384], in_=xs[:, lo + 256:lo + 384], func=AF.Sigmoid)
            for j in range(3):
                X.pe.transpose(out=ps_tr[:, j, :], in_=la[:, j * 128:(j + 1) * 128], identity=K["ident_bf"][:])
            X.dve.tensor_copy(out=lT[:], in_=ps_tr[:, 0:3, :])
            X.dve.tensor_tensor(out=t1[:], in0=kr, in1=kkb[:], op=ALU.mult)
            t1v = t1[:].rearrange("p (h k) -> p h k", h=RH)
            t2v = t2[:].rearrange("p (h k) -> p h k", h=RH)
            X.dve.tensor_tensor(out=t2[:], in0=t1[:], in1=t1[:], op=ALU.mult)
            X.dve.tensor_reduce(out=ssh[:], in_=t2v, axis=AX.X, op=ALU.add)
            X.act.activation(out=ssh[:], in_=ssh[:], func=AF.Sqrt)
            X.dve.tensor_scalar(out=ssh[:], in0=ssh[:], scalar1=1e-12, scalar2=None, op0=ALU.max)
            X.dve.reciprocal(out=ssh[:], in_=ssh[:])
            X.dve.tensor_tensor(out=kkn[:].rearrange("p (h k) -> p h k", h=RH), in0=t1v,
                                in1=ssh[:].unsqueeze(2).to_broadcast([128, RH, 64]), op=ALU.mult)
            X.act.dma_start(out=A["RKK"][r0:r0 + 128, :], in_=kkn[:])
            for n0 in range(0, RW, 512):
                X.pe.matmul(out=ps_w[:, n0:min(RW, n0 + 512)], lhsT=lT[:, 2, :], rhs=g2sb[:, n0:min(RW, n0 + 512)], start=True, stop=True)
            X.act.activation(out=t2[:], in_=ps_w[:], func=AF.Copy)
            X.act.dma_start(out=A["RGATE"][r0:r0 + 128, :], in_=t2[:])
            for d in range(2):
                p0 = d * 64
                for n0 in range(0, RW, 512):
                    X.pe.matmul(out=ps_w[:, n0:min(RW, n0 + 512)], lhsT=lT[p0:p0 + 64, 0, :], rhs=w2sb[p0:p0 + 64, n0:min(RW, n0 + 512)],
                                start=True, stop=True)
                    X.pe.matmul(out=ps_a[:, n0:min(RW, n0 + 512)], lhsT=lT[p0:p0 + 64, 1, :], rhs=a2sb[p0:p0 + 64, n0:min(RW, n0 + 512)],
                                start=True, stop=True)
                X.dve.tensor_tensor(out=t1[:], in0=ps_w[:], in1=w0b[:, d, :], op=ALU.add)
                X.act.activation(out=t1[:], in_=t1[:], func=AF.Sigmoid)
                X.dve.tensor_scalar(out=t1[:], in0=t1[:], scalar1=-0.6065306597126334, scalar2=None, op0=ALU.mult)
                X.act.dma_start(out=A["RLW%d" % d][r0:r0 + 128, :], in_=t1[:])
                X.dve.tensor_tensor(out=asg[:], in0=ps_a[:], in1=a0b[:, d, :], op=ALU.add)
                X.act.activation(out=asg[:], in_=asg[:], func=AF.Sigmoid)
                X.dve.tensor_tensor(out=t2[:], in0=kkn[:], in1=asg[:], op=ALU.mult)
                X.act.dma_start(out=A["RB%d" % d][r0:r0 + 128, :], in_=t2[:])
                X.dve.scalar_tensor_tensor(out=t1[:], in0=asg[:], scalar=-1.0, in1=kab[:], op0=ALU.add, op1=ALU.mult)
                X.dve.scalar_tensor_tensor(out=t1[:], in0=t1[:], scalar=1.0, in1=kr, op0=ALU.add, op1=ALU.mult)
                X.act.dma_start(out=A["RKD%d" % d][r0:r0 + 128, :], in_=t1[:])
                X.dve.tensor_tensor(out=t2[:], in0=t1[:], in1=rr, op=ALU.mult)
                X.dve.tensor_tensor(out=t2[:], in0=t2[:], in1=rkb[:], op=ALU.mult)
                X.dve.tensor_reduce(out=ssh[:], in_=t2v, axis=AX.X, op=ALU.add)
                vv = vr.rearrange("p (h k) -> p h k", h=RH)
                bv = bon[:].rearrange("p (h k) -> p h k", h=RH)
                sb_ = ssh[:].unsqueeze(2).to_broadcast([128, RH, 64])
                if d == 0:
                    X.dve.tensor_tensor(out=bv, in0=vv, in1=sb_, op=ALU.mult)
                else:
                    X.dve.tensor_tensor(out=t2v, in0=vv, in1=sb_, op=ALU.mult)
                    X.dve.tensor_tensor(out=bon[:], in0=bon[:], in1=t2[:], op=ALU.add)
            X.act.dma_start(out=A["RBON"][r0:r0 + 128, :], in_=bon[:])
        X.flush()


def rwkv_scan(X, C, K, A, l, seq, d):
    tok0, T, is_s, si = seq
    NT = T // 128
    RW, RH = C.R_W, C.RH
    NP = RH // 2
    MK = K["masks"]
    ident = K["ident_f"]
    with ExitStack() as st:
        H = X.sb(st, "sH", [128, NP, 64])
        Hb = X.sb(st, "sHb", [128, NP, 64], BF16)
        ps_L = X.ps(st, "spL", [128, max(RW, 512)])
        ps_tr1 = X.ps(st, "sptr", [128, 8, 128], BF16)
        ps_tr = [ps_tr1, ps_tr1]
        ps_sc2 = [X.ps(st, "spsc%d" % i, [128, 512]) for i in range(2)]
        ps_ivt2 = [X.ps(st, "spiv%d" % i, [128, 4, 128]) for i in range(2)]
        ps_iv = [ps_ivt2[0][:, i, :] for i in range(4)]
        ps_smt = X.ps(st, "spsm", [128, 512])
        ps_pc = ps_smt[:, 448:448 + 2 * NP]
        ps_hu = ps_L[:, 0:512].rearrange("p (a b c) -> p a b c", a=4, b=2)
        if is_s and not DBG.get("no_init"):
            Sv = X.sb(st, "sSv", [64, RH, 64])
            X.sp.dma_start(out=Sv[:], in_=A["st_S"][l, d].rearrange("h v k -> v h k"))
            for hp in range(NP):
                pst = ps_iv[hp % 4]
                X.pe.transpose(out=pst[:, 0:64], in_=Sv[:, 2 * hp:2 * hp + 2, :].rearrange("p a k -> p (a k)"),
                               identity=ident[0:64, 0:64])
                X.dve.tensor_copy(out=H[:, hp, :], in_=pst[:, 0:64])
        else:
            X.dve.memset(ap=H[:], constant=0.0)
        X.act.activation(out=Hb[:], in_=H[:], func=AF.Copy)
        ld = {k: [X.sb(st, "s%s%d" % (k, i), [128, RW]) for i in range(2)] for k in ("r", "v", "kk", "lw", "kd", "bv")}
        src = {"r": "RR", "v": "RV", "kk": "RKK", "lw": "RLW%d" % d, "kd": "RKD%d" % d, "bv": "RB%d" % d}
        Pin = X.sb(st, "sPin", [128, RW])
        Pinv = X.sb(st, "sPinv", [128, RW])
        Pex = X.sb(st, "sPex", [128, RW])
        tok = {k: X.sb(st, "sT%s" % k, [128, RW], BF16) for k in ("a", "r", "b", "k", "v")}
        arT = X.sb(st, "sarT", [128, NP, 2, 128], BF16)
        bkT = X.sb(st, "sbkT", [128, NP, 2, 128], BF16)
        pct = X.sb(st, "spct", [128, NP])
        Xf2 = [X.sb(st, "sXf%d" % i, [128, 128]) for i in range(2)]
        Xl2 = [X.sb(st, "sXl%d" % i, [128, 6, 128]) for i in range(2)]
        sc3 = [X.sb(st, "ssc3%d" % i, [128, 3, 128], BF16) for i in range(2)]
        Dm2 = [X.sb(st, "sD%d" % i, [128, 128]) for i in range(2)]
        DT2 = [X.sb(st, "sDT%d" % i, [128, 128]) for i in range(2)]
        T12 = [X.sb(st, "sT1%d" % i, [128, 128]) for i in range(2)]
        NTb2 = [X.sb(st, "sNTb%d" % i, [128, 128], BF16) for i in range(2)]
        rhsb2 = [X.sb(st, "srhsb%d" % i, [128, 64], BF16) for i in range(2)]
        Ub = X.sb(st, "sUb", [128, RH, 64], BF16)
        Yt = [X.sb(st, "sYt%d" % i, [128, RW]) for i in range(2)]
        order = range(NT) if d == 0 else range(NT - 1, -1, -1)
        for ci, c in enumerate(order):
            b = ci % 2
            r0 = tok0 + c * 128
            if DBG.get("pre0"):
                continue
            for k in ld:
                X.sp.dma_start(out=ld[k][b][:], in_=A[src[k]][r0:r0 + 128, :])
            if DBG.get("pre05"):
                continue
            lw = ld["lw"][b]
            for n0 in range(0, RW, 512):
                X.pe.matmul(out=ps_L[:, n0:min(RW, n0 + 512)], lhsT=MK[:, M_INCL + d, :], rhs=lw[:, n0:min(RW, n0 + 512)], start=True, stop=True)
            if DBG.get("p1a"):
                continue
            X.act.activation(out=Pin[:], in_=ps_L[:, 0:RW], func=AF.Exp)
            if DBG.get("p1b"):
                continue
            X.act.activation(out=Pinv[:], in_=ps_L[:, 0:RW], func=AF.Exp, scale=-1.0)
            if DBG.get("p1c"):
                continue
            X.act.activation(out=Pex[:], in_=lw[:], func=AF.Exp, scale=-1.0)
            X.dve.tensor_tensor(out=Pex[:], in0=Pex[:], in1=Pin[:], op=ALU.mult)
            if DBG.get("pre1"):
                continue
            X.dve.tensor_tensor(out=tok["r"][:], in0=ld["r"][b][:], in1=Pin[:], op=ALU.mult)
            X.dve.scalar_tensor_tensor(out=tok["a"][:], in0=ld["kk"][b][:], scalar=-1.0, in1=Pex[:], op0=ALU.mult, op1=ALU.mult)
            X.dve.tensor_tensor(out=tok["b"][:], in0=ld["bv"][b][:], in1=Pinv[:], op=ALU.mult)
            X.dve.tensor_tensor(out=tok["k"][:], in0=ld["kd"][b][:], in1=Pinv[:], op=ALU.mult)
            X.act.activation(out=tok["v"][:], in_=ld["v"][b][:], func=AF.Copy)
            if DBG.get("pre2"):
                continue
            for ti_, (nm, dstT, slot) in enumerate((("a", arT, 0), ("r", arT, 1), ("b", bkT, 0), ("k", bkT, 1))):
                pt = ps_tr[ti_ % 2]
                for hp in range(NP):
                    X.pe.transpose(out=pt[:, hp, :], in_=tok[nm][:, hp * 128:(hp + 1) * 128], identity=K["ident_bf"][:])
                if ti_ % 2 == 0:
                    X.dve.tensor_copy(out=dstT[:, :, slot, :], in_=pt[:, 0:NP, :])
                else:
                    X.act.activation(out=dstT[:, :, slot, :], in_=pt[:, 0:NP, :], func=AF.Copy)
            if DBG.get("pre3"):
                continue
            e0 = 126 if d == 0 else 0
            ecol = 1 if d == 0 else 0
            for hp in range(NP):
                X.pe.matmul(out=ps_pc[:, 2 * hp:2 * hp + 2], lhsT=Pin[:, hp * 128:(hp + 1) * 128], rhs=ident[:, e0:e0 + 2],
                            start=True, stop=True)
            X.act.activation(out=pct[:], in_=ps_pc.rearrange("p (a b) -> p a b", b=2)[:, :, ecol], func=AF.Copy)
            def head_ops(h):
                hp, h2 = divmod(h, 2)
                par = h % 2
                p0 = h2 * 64
                s3 = sc3[par]
                Xf, Xl, Dm, DT, T1, NTb, rhsb = Xf2[par], Xl2[par], Dm2[par], DT2[par], T12[par], NTb2[par], rhsb2[par]
                ps_sc = ps_sc2[par]
                piv = [ps_ivt2[par][:, i, :] for i in range(4)]
                psm = [ps_smt[:, (par * 3 + i) * 64:(par * 3 + i + 1) * 64] for i in range(3)]
                aT_h = arT[p0:p0 + 64, hp, 0, :]
                rT_h = arT[p0:p0 + 64, hp, 1, :]
                bT_h = bkT[p0:p0 + 64, hp, 0, :]
                kT_h = bkT[p0:p0 + 64, hp, 1, :]
                ar_h = arT[p0:p0 + 64, hp, :, :].rearrange("p a t -> p (a t)")
                X.pe.matmul(out=ps_sc[:, 0:256], lhsT=bT_h, rhs=ar_h, start=True, stop=True)
                X.pe.matmul(out=ps_sc[:, 256:512], lhsT=kT_h, rhs=ar_h, start=True, stop=True)
                X.pe.matmul(out=piv[0][:], lhsT=aT_h, rhs=bT_h, start=True, stop=True)
                X.dve.tensor_tensor(out=Xf[:], in0=ps_sc[:, 0:128], in1=MK[:, M_STRICT + d, :], op=ALU.mult)
                X.dve.tensor_tensor(out=s3[:], in0=ps_sc[:, 128:512].rearrange("p (a t) -> p a t", a=3),
                                    in1=MK[:, M_ISI + 3 * d:M_ISI + 3 * d + 3, :], op=ALU.mult)
                X.dve.tensor_tensor(out=Dm[:], in0=piv[0][:], in1=MK[:, M_L0T + d, :], op=ALU.mult)
                X.dve.tensor_tensor(out=Dm[:], in0=Dm[:], in1=ident[:], op=ALU.add)
                X.dve.tensor_tensor(out=DT[:], in0=Xf[:], in1=MK[:, M_LVL + 7 * d, :], op=ALU.mult)
                X.dve.tensor_tensor(out=DT[:], in0=DT[:], in1=ident[:], op=ALU.add)
                X.dve.tensor_tensor(out=Xl[:], in0=Xf[:].unsqueeze(1).to_broadcast([128, 6, 128]),
                                    in1=MK[:, M_LVL + 7 * d + 1:M_LVL + 7 * d + 7, :], op=ALU.mult)
                for lv in range(1, 7):
                    X.pe.matmul(out=piv[1][:], lhsT=Xl[:, lv - 1, :], rhs=Dm[:], start=True, stop=True)
                    X.dve.tensor_copy(out=T1[:], in_=piv[1][:])
                    if lv < 6:
                        X.pe.matmul(out=piv[2][:], lhsT=DT[:], rhs=T1[:], start=True, stop=True)
                    X.pe.matmul(out=piv[3][:], lhsT=T1[:], rhs=DT[:], start=True, stop=True)
                    if lv < 6:
                        X.dve.tensor_tensor(out=Dm[:], in0=Dm[:], in1=piv[2][:], op=ALU.add)
                    X.dve.tensor_tensor(out=DT[:], in0=DT[:], in1=piv[3][:], op=ALU.add)
                X.act.activation(out=NTb[:], in_=DT[:], func=AF.Copy)
                vb_h = tok["v"][:, h * 64:(h + 1) * 64]
                X.pe.matmul(out=psm[0], lhsT=aT_h, rhs=Hb[p0:p0 + 64, hp, :], start=True, stop=False)
                X.pe.matmul(out=psm[0], lhsT=s3[:, 1, :], rhs=vb_h, start=False, stop=True)
                X.act.activation(out=rhsb[:], in_=psm[0], func=AF.Copy)
                X.pe.matmul(out=psm[1], lhsT=NTb[:], rhs=rhsb[:], start=True, stop=True)
                X.act.activation(out=Ub[:, h, :], in_=psm[1], func=AF.Copy)
                X.pe.matmul(out=psm[2], lhsT=rT_h, rhs=Hb[p0:p0 + 64, hp, :], start=True, stop=False)
                X.pe.matmul(out=psm[2], lhsT=s3[:, 0, :], rhs=Ub[:, h, :], start=False, stop=False)
                X.pe.matmul(out=psm[2], lhsT=s3[:, 2, :], rhs=vb_h, start=False, stop=True)
                X.act.activation(out=Yt[b][:, h * 64:(h + 1) * 64], in_=psm[2], func=AF.Copy)

            for h in range(0, RH, 2):
                recs = []
                for hh in (h, h + 1):
                    X.P.recording = []
                    head_ops(hh)
                    recs.append(X.P.recording)
                    X.P.recording = None
                ptr = [0, 0]
                while any(ptr[k_] < len(recs[k_]) for k_ in range(2)):
                    for k_ in range(2):
                        r_ = recs[k_]
                        if ptr[k_] < len(r_):
                            r_[ptr[k_]][0]()
                            ptr[k_] += 1
                            while ptr[k_] < len(r_) and r_[ptr[k_]][1]:
                                r_[ptr[k_]][0]()
                                ptr[k_] += 1
            X.act.dma_start(out=A["RY%d" % d][r0:r0 + 128, :], in_=Yt[b][:])
            for g4 in range(0, NP if not DBG.get("no_hupd") else 0, 4):
                for j in range(min(4, NP - g4)):
                    hp = g4 + j
                    for h2 in range(2):
                        h = 2 * hp + h2
                        X.pe.matmul(out=ps_hu[:, j, h2, :], lhsT=tok["b"][:, hp * 128:(hp + 1) * 128], rhs=Ub[:, h, :],
                                    start=True, stop=False)
                        X.pe.matmul(out=ps_hu[:, j, h2, :], lhsT=tok["k"][:, hp * 128:(hp + 1) * 128],
                                    rhs=tok["v"][:, h * 64:(h + 1) * 64], start=False, stop=True)
                n4 = min(4, NP - g4)
                X.dve.tensor_tensor(out=H[0:64, g4:g4 + n4, :], in0=H[0:64, g4:g4 + n4, :], in1=ps_hu[0:64, 0:n4, 0, :], op=ALU.add)
                X.dve.tensor_tensor(out=H[64:128, g4:g4 + n4, :], in0=H[64:128, g4:g4 + n4, :], in1=ps_hu[64:128, 0:n4, 1, :],
                                    op=ALU.add)
            X.dve.tensor_tensor(out=H[:], in0=H[:], in1=pct[:].unsqueeze(2).to_broadcast([128, NP, 64]), op=ALU.mult)
            X.act.activation(out=Hb[:], in_=H[:], func=AF.Copy)
        if not is_s and not DBG.get("no_fin"):
            So = X.sb(st, "sSo", [64, NP, 128])
            for hp in range(NP):
                pst = ps_iv[hp % 4]
                X.pe.transpose(out=pst[0:64, :], in_=H[:, hp, :], identity=ident[:])
                X.dve.tensor_copy(out=So[:, hp, :], in_=pst[0:64, :])
            X.act.dma_start(out=A["o_S"][si, l, d].rearrange("(hp h2) v k -> v hp h2 k", h2=2),
                            in_=So[:].rearrange("p a (b k) -> p a b k", b=2))
        X.flush()


def rwkv_final(X, C, K, A, l, seq):
    tok0, T, is_s, si = seq
    NT = T // 128
    RW, RH = C.R_W, C.RH
    with ExitStack() as st:
        lnw = X.sb(st, "flnw", [128, RW])
        lnb = X.sb(st, "flnb", [128, RW])
        load_rows(X, lnw[:], A["ln_x_rwkv"][l, 0, :])
        load_rows(X, lnb[:], A["ln_x_rwkv"][l, 1, :])
        y0 = [X.sb(st, "fy0%d" % i, [128, RW]) for i in range(2)]
        y1 = [X.sb(st, "fy1%d" % i, [128, RW]) for i in range(2)]
        bo = [X.sb(st, "fbo%d" % i, [128, RW]) for i in range(2)]
        ga = [X.sb(st, "fga%d" % i, [128, RW]) for i in range(2)]
        t1 = X.sb(st, "ft1", [128, RW])
        mean = X.sb(st, "fmean", [128, RH])
        var = X.sb(st, "fvar", [128, RH])
        for i in range(NT):
            b = i % 2
            r0 = tok0 + i * 128
            X.sp.dma_start(out=y0[b][:], in_=A["RY0"][r0:r0 + 128, :])
            X.sp.dma_start(out=y1[b][:], in_=A["RY1"][r0:r0 + 128, :])
            X.sp.dma_start(out=bo[b][:], in_=A["RBON"][r0:r0 + 128, :])
            X.sp.dma_start(out=ga[b][:], in_=A["RGATE"][r0:r0 + 128, :])
            y = y0[b]
            yv = y[:].rearrange("p (h k) -> p h k", h=RH)
            tv = t1[:].rearrange("p (h k) -> p h k", h=RH)
            X.dve.tensor_tensor(out=y[:], in0=y[:], in1=y1[b][:], op=ALU.add)
            X.dve.tensor_reduce(out=mean[:], in_=yv, axis=AX.X, op=ALU.add)
            X.dve.tensor_scalar(out=mean[:], in0=mean[:], scalar1=1.0 / 64, scalar2=None, op0=ALU.mult)
            X.dve.tensor_tensor(out=yv, in0=yv, in1=mean[:].unsqueeze(2).to_broadcast([128, RH, 64]), op=ALU.subtract)
            X.dve.tensor_tensor(out=t1[:], in0=y[:], in1=y[:], op=ALU.mult)
            X.dve.tensor_reduce(out=var[:], in_=tv, axis=AX.X, op=ALU.add)
            X.dve.tensor_scalar(out=var[:], in0=var[:], scalar1=1.0 / 64, scalar2=64e-5, op0=ALU.mult, op1=ALU.add)
            rsqrt_ip(X, var[:])
            X.dve.tensor_tensor(out=yv, in0=yv, in1=var[:].unsqueeze(2).to_broadcast([128, RH, 64]), op=ALU.mult)
            X.dve.tensor_tensor(out=y[:], in0=y[:], in1=lnw[:], op=ALU.mult)
            X.dve.tensor_tensor(out=y[:], in0=y[:], in1=lnb[:], op=ALU.add)
            X.dve.tensor_tensor(out=y[:], in0=y[:], in1=bo[b][:], op=ALU.add)
            X.dve.tensor_tensor(out=y[:], in0=y[:], in1=ga[b][:], op=ALU.mult)
            X.act.dma_start(out=A["MIX"][r0:r0 + 128, C.ATT_W + C.MV_W:C.ATT_W + C.MV_W + RW], in_=y[:])
        X.flush()


RWKV_STAGE = [3]
DBG = {}


def rwkv_seq(X, C, K, A, l, seq):
    rwkv_prep(X, C, K, A, l, seq)
    if RWKV_STAGE[0] < 2:
        return
    for d in range(2):
        rwkv_scan(X, C, K, A, l, seq, d)
    if RWKV_STAGE[0] < 3:
        return
    rwkv_final(X, C, K, A, l, seq)
```
